# Optimizing a Trainium2 kernel written in Bass

```python
import jax, jax.numpy as jnp
from jax import lax
import numpy as np

D_MODEL = 1024
BATCH = 16
SEQ = 2048
DEPTH = 1
DEC_BATCH = 16
DEC_SEQ = 64
PAST_LEN = 4096

CHUNK = 64
D_MIX = D_MODEL
D_CONV = D_MIX // 2
CONV_WIDTH = 31
HG_HEADS = 4
HG_DK = (D_MIX - D_CONV) // HG_HEADS
HG_DV = HG_DK
D_HG = HG_HEADS * HG_DK
D_IN = 2 * D_CONV + 4 * D_HG
D_FF = 2816
EPS = 1e-6

kernel_name = "conformer_conv_hgrn2_hybrid_step"


def _rmsnorm(x, g):
    xf = x.astype(jnp.float32)
    y = xf * lax.rsqrt(jnp.mean(xf * xf, axis=-1, keepdims=True) + EPS)
    return (y * g.astype(jnp.float32)).astype(x.dtype)


def _layernorm(x, g, b):
    xf = x.astype(jnp.float32)
    mu = jnp.mean(xf, axis=-1, keepdims=True)
    var = jnp.mean(jnp.square(xf - mu), axis=-1, keepdims=True)
    y = (xf - mu) * lax.rsqrt(var + EPS)
    return (y * g.astype(jnp.float32) + b.astype(jnp.float32)).astype(x.dtype)


def _swiglu(h, w1, w3, w2):
    return (jax.nn.silu(h @ w1) * (h @ w3)) @ w2


def _conv_module(u, buf, dw_w, dw_b, ln_g, ln_b):
    xp = jnp.concatenate([buf.astype(u.dtype), u], axis=1)
    y = lax.conv_general_dilated(
        xp, dw_w[:, None, :].astype(u.dtype), window_strides=(1,), padding='VALID',
        dimension_numbers=('NWC', 'WIO', 'NWC'), feature_group_count=D_CONV)
    y = y + dw_b.astype(u.dtype)
    y = jax.nn.silu(_layernorm(y, ln_g, ln_b))
    return y, xp[:, -(CONV_WIDTH - 1):]


def _chunk_scan(q, k, logf, v, s0):
    B, T, H, DK = q.shape
    L = min(CHUNK, T)
    n = T // L

    def split(a):
        return a.reshape(B, n, L, *a.shape[2:]).swapaxes(0, 1)

    causal = jnp.tril(jnp.ones((L, L), dtype=bool))[None, :, :, None, None]

    def step(S, inp):
        qc, kc, fc, vc = inp
        b = jnp.cumsum(fc, axis=1)
        diff = b[:, :, None] - b[:, None, :]
        decay = jnp.exp(jnp.where(causal, diff, -jnp.inf))
        A = jnp.einsum('bthk,bshk,btshk->bhts', qc, kc, decay)
        o = jnp.einsum('bhts,bshv->bthv', A, vc) + jnp.einsum('bthk,bhkv->bthv', qc * jnp.exp(b), S)
        bl = b[:, -1]
        S_new = jnp.exp(bl)[..., None] * S + jnp.einsum('bshk,bshv->bhkv', kc * jnp.exp(bl[:, None] - b), vc)
        return S_new, o

    S, o = lax.scan(step, s0, (split(q), split(k), split(logf), split(v)))
    o = o.swapaxes(0, 1).reshape(B, T, H, v.shape[-1])
    return o, S


def _hgrn2(q_raw, f_raw, i_raw, g_raw, s0, lb, gn):
    B, T, _ = q_raw.shape
    f32 = jnp.float32
    q = jax.nn.silu(q_raw.astype(f32)).reshape(B, T, HG_HEADS, HG_DK)
    fr = f_raw.astype(f32).reshape(B, T, HG_HEADS, HG_DK)
    lbh = lb.reshape(HG_HEADS, HG_DK)
    logf = jnp.log(lbh + (1.0 - lbh) * jax.nn.sigmoid(fr))
    k = (1.0 - lbh) * jax.nn.sigmoid(-fr)
    v = i_raw.astype(f32).reshape(B, T, HG_HEADS, HG_DV)
    o, S = _chunk_scan(q, k, logf, v, s0.astype(f32))
    o = o * lax.rsqrt(jnp.mean(o * o, axis=-1, keepdims=True) + EPS)
    o = o.reshape(B, T, D_HG) * gn.astype(f32) * jax.nn.silu(g_raw.astype(f32))
    return o.astype(q_raw.dtype), S


def _layer(x, conv_buf, s0, lb, n1, f1a, f1b, f1c, nm, w_in, dw_w, dw_b, ln_g, ln_b, gn, w_out, n2, f2a, f2b, f2c):
    x = x + 0.5 * _swiglu(_rmsnorm(x, n1), f1a, f1b, f1c)
    h = _rmsnorm(x, nm)
    p = h @ w_in
    a, gt, qr, fr, ir, gr = jnp.split(
        p, [D_CONV, 2 * D_CONV, 2 * D_CONV + D_HG, 2 * D_CONV + 2 * D_HG, 2 * D_CONV + 3 * D_HG], axis=-1)
    u = a * jax.nn.sigmoid(gt)
    c, new_buf = _conv_module(u, conv_buf, dw_w, dw_b, ln_g, ln_b)
    r, S = _hgrn2(qr, fr, ir, gr, s0, lb, gn)
    x = x + jnp.concatenate([c, r], axis=-1) @ w_out
    x = x + 0.5 * _swiglu(_rmsnorm(x, n2), f2a, f2b, f2c)
    return x, new_buf, S


def setup_inputs(seed: int = 0) -> dict:
    key = jax.random.key(seed)
    ks = jax.random.split(key, 24)
    f32 = jnp.float32

    def nrm(k, shape, scale):
        return jax.random.normal(k, shape, f32) * scale

    def gain(k, shape):
        return 1.0 + 0.02 * jax.random.normal(k, shape, f32)

    return {
        "x_prompt": nrm(ks[0], (BATCH, SEQ, D_MODEL), 1.0),
        "x_sample": nrm(ks[1], (DEC_BATCH, DEC_SEQ, D_MODEL), 1.0),
        "state_conv": nrm(ks[2], (DEPTH, DEC_BATCH, CONV_WIDTH - 1, D_CONV), 0.5),
        "state_hgrn": nrm(ks[3], (DEPTH, DEC_BATCH, HG_HEADS, HG_DK, HG_DV), 0.3),
        "ffn1_norm": gain(ks[4], (DEPTH, D_MODEL)),
        "ffn1_w1": nrm(ks[5], (DEPTH, D_MODEL, D_FF), D_MODEL ** -0.5),
        "ffn1_w3": nrm(ks[6], (DEPTH, D_MODEL, D_FF), D_MODEL ** -0.5),
        "ffn1_w2": nrm(ks[7], (DEPTH, D_FF, D_MODEL), D_FF ** -0.5),
        "mix_norm": gain(ks[8], (DEPTH, D_MODEL)),
        "w_in": nrm(ks[9], (DEPTH, D_MODEL, D_IN), D_MODEL ** -0.5),
        "conv_dw_w": nrm(ks[10], (DEPTH, CONV_WIDTH, D_CONV), CONV_WIDTH ** -0.5),
        "conv_dw_b": nrm(ks[11], (DEPTH, D_CONV), 0.02),
        "conv_ln_g": gain(ks[12], (DEPTH, D_CONV)),
        "conv_ln_b": nrm(ks[13], (DEPTH, D_CONV), 0.02),
        "hg_lb_logits": nrm(ks[14], (DEPTH + 1, D_HG), 0.1),
        "hg_gnorm": gain(ks[15], (DEPTH, D_HG)),
        "w_out": nrm(ks[16], (DEPTH, D_MIX, D_MODEL), D_MIX ** -0.5),
        "ffn2_norm": gain(ks[17], (DEPTH, D_MODEL)),
        "ffn2_w1": nrm(ks[18], (DEPTH, D_MODEL, D_FF), D_MODEL ** -0.5),
        "ffn2_w3": nrm(ks[19], (DEPTH, D_MODEL, D_FF), D_MODEL ** -0.5),
        "ffn2_w2": nrm(ks[20], (DEPTH, D_FF, D_MODEL), D_FF ** -0.5),
        "final_norm": gain(ks[21], (D_MODEL,)),
    }


def reference(x_prompt, x_sample, state_conv, state_hgrn, ffn1_norm, ffn1_w1, ffn1_w3, ffn1_w2,
              mix_norm, w_in, conv_dw_w, conv_dw_b, conv_ln_g, conv_ln_b, hg_lb_logits, hg_gnorm,
              w_out, ffn2_norm, ffn2_w1, ffn2_w3, ffn2_w2, final_norm):
    lb_all = jnp.cumsum(jax.nn.softmax(hg_lb_logits.astype(jnp.float32), axis=0), axis=0)
    yp, ys = x_prompt, x_sample
    conv_p, hgrn_p, conv_s, hgrn_s = [], [], [], []
    for l in range(DEPTH):
        lw = (ffn1_norm[l], ffn1_w1[l], ffn1_w3[l], ffn1_w2[l], mix_norm[l], w_in[l],
              conv_dw_w[l], conv_dw_b[l], conv_ln_g[l], conv_ln_b[l], hg_gnorm[l], w_out[l],
              ffn2_norm[l], ffn2_w1[l], ffn2_w3[l], ffn2_w2[l])
        buf0 = jnp.zeros((x_prompt.shape[0], CONV_WIDTH - 1, D_CONV), x_prompt.dtype)
        s00 = jnp.zeros((x_prompt.shape[0], HG_HEADS, HG_DK, HG_DV), jnp.float32)
        yp, bp, sp = _layer(yp, buf0, s00, lb_all[l], *lw)
        ys, bs, ss = _layer(ys, state_conv[l], state_hgrn[l], lb_all[l], *lw)
        conv_p.append(bp.astype(state_conv.dtype))
        hgrn_p.append(sp.astype(state_hgrn.dtype))
        conv_s.append(bs.astype(state_conv.dtype))
        hgrn_s.append(ss.astype(state_hgrn.dtype))
    y_prompt = _rmsnorm(yp, final_norm)
    y_sample = _rmsnorm(ys, final_norm)
    new_conv_prompt = jnp.stack(conv_p, axis=0)
    new_hgrn_prompt = jnp.stack(hgrn_p, axis=0)
    new_conv_sample = jnp.stack(conv_s, axis=0)
    new_hgrn_sample = jnp.stack(hgrn_s, axis=0)
    return (y_prompt, y_sample, new_conv_prompt, new_hgrn_prompt, new_conv_sample, new_hgrn_sample)
```

```python
import contextlib
import numpy as np
import concourse.bass as bass
import concourse.mybir as mybir
from concourse.bass_utils import run_bass_kernel_spmd

F32 = mybir.dt.float32
BF16 = mybir.dt.bfloat16
U8 = mybir.dt.uint8
AF = mybir.ActivationFunctionType
ALU = mybir.AluOpType

CELL = 256
DT_SIZE = {F32: 4, BF16: 2, U8: 1}

D = 1024
DFF = 2816
NJ = 22
DC = 512
EPS = 1e-6
N_CORES = 8


class Cell:
    __slots__ = ("last_w", "readers")

    def __init__(self):
        self.last_w = None
        self.readers = {}


class Tile:
    def __init__(self, ctx, name, shape, dtype, addr):
        self.ctx = ctx
        self.name = name
        self.shape = list(shape)
        self.dtype = dtype
        self.esz = DT_SIZE[dtype]
        self.free = int(np.prod(shape[1:]))
        self.nbytes = self.free * self.esz
        self.addr = addr
        v = ctx.arena[0:shape[0], addr:addr + self.nbytes].bitcast(dtype)
        if len(shape) == 3:
            v = v.rearrange("p (a b) -> p a b", b=shape[2])
        elif len(shape) == 4:
            v = v.rearrange("p (a b c) -> p a b c", b=shape[2], c=shape[3])
        self.ap = v

    def __getitem__(self, k):
        return self.ap[k]

    def cells(self, lo=None, n=None):
        if lo is None:
            lo, n = 0, self.free
        b0 = self.addr + lo * self.esz
        b1 = self.addr + (lo + n) * self.esz
        return self.ctx.cells[b0 // CELL:(b1 - 1) // CELL + 1]


class Rot:
    def __init__(self, tiles):
        self.tiles = tiles
        self.i = 0

    def next(self):
        t = self.tiles[self.i % len(self.tiles)]
        self.i += 1
        return t


class Ctx:
    ENGS = ["pe", "act", "dve", "pool", "sp"]

    def __init__(self, nc, arena_bytes, n_dma_sems):
        self.nc = nc
        self.arena_bytes = arena_bytes
        self.arena_t = nc.alloc_sbuf_tensor("arena", [128, arena_bytes], U8)
        self.arena = self.arena_t[:, :]
        self.cells = [Cell() for _ in range(arena_bytes // CELL + 1)]
        self.ptr = 0
        self.streams = {e: [] for e in self.ENGS}
        self.cnt = {e: 0 for e in self.ENGS}
        self.clock = {e: {} for e in self.ENGS}
        self.evclock = {}
        self.dma_pool = {q: ["d_%s_%d" % (q, i) for i in range(n)] for q, n in n_dma_sems.items()}
        self.dma_cnt = {}
        for q in self.dma_pool:
            for s in self.dma_pool[q]:
                self.dma_cnt[s] = 0
        self.dma_rr = {q: 0 for q in self.dma_pool}
        self.ninst = 0

    def alloc(self, name, shape, dtype, at=None):
        esz = DT_SIZE[dtype]
        nbytes = int(np.prod(shape[1:])) * esz
        if at is None:
            addr = (self.ptr + CELL - 1) // CELL * CELL
            self.ptr = addr + nbytes
            assert self.ptr <= self.arena_bytes, "SBUF arena overflow at %s: %d" % (name, self.ptr)
        else:
            addr = at
            assert addr + nbytes <= self.arena_bytes
        return Tile(self, name, shape, dtype, addr)

    def _merge_clock(self, eng, k, v):
        ck = self.clock[eng]
        ec = self.evclock.get((k, v))
        if ec:
            for kk, vv in ec.items():
                if ck.get(kk, 0) < vv:
                    ck[kk] = vv
        if ck.get(k, 0) < v:
            ck[k] = v

    def _deps(self, eng, reads, writes, is_dma):
        deps = {}

        def need(ev, kind):
            if ev is None:
                return
            k, v = ev
            if k == eng and not is_dma and eng == "pe":
                return
            if deps.get(k, 0) < v:
                deps[k] = v

        for c in reads:
            need(c.last_w, "raw")
        for c in writes:
            need(c.last_w, "waw")
            for k, v in c.readers.items():
                need((k, v), "war")
        ck = self.clock[eng]
        waits = [(k, v) for k, v in deps.items() if ck.get(k, 0) < v]
        for k, v in waits:
            self._merge_clock(eng, k, v)
        return waits

    def _mark(self, ev, reads, writes):
        k, v = ev
        for c in reads:
            if c.readers.get(k, 0) < v:
                c.readers[k] = v
        for c in writes:
            c.last_w = ev
            c.readers = {}

    @staticmethod
    def _flat(lst):
        out = []
        for x in lst:
            if isinstance(x, Cell):
                out.append(x)
            elif isinstance(x, Tile):
                out.extend(x.cells())
            else:
                out.extend(Ctx._flat(x))
        return out

    def op(self, eng, fns, reads=(), writes=()):
        if callable(fns):
            fns = [fns]
        reads = self._flat(reads)
        writes = self._flat(writes)
        waits = self._deps(eng, reads, writes, False)
        self.cnt[eng] += 1
        ev = (eng, self.cnt[eng])
        self.evclock[ev] = dict(self.clock[eng])
        self._mark(ev, reads, writes)
        self.streams[eng].append(("op", waits, fns, None))
        self.ninst += len(fns) + len(waits)
        return ev

    def dma(self, q, out_ap, in_ap, reads=(), writes=()):
        reads = self._flat(reads)
        writes = self._flat(writes)
        pool = self.dma_pool[q]
        sem = pool[self.dma_rr[q] % len(pool)]
        self.dma_rr[q] += 1
        waits = self._deps(q, reads, writes, True)
        prev = self.dma_cnt[sem]
        if prev > 0 and self.clock[q].get(sem, 0) < 16 * prev:
            waits.append((sem, 16 * prev))
            self._merge_clock(q, sem, 16 * prev)
        self.dma_cnt[sem] += 1
        ev = (sem, 16 * self.dma_cnt[sem])
        self.evclock[ev] = dict(self.clock[q])
        self._mark(ev, reads, writes)
        self.streams[q].append(("dma", waits, [lambda e, o=out_ap, i=in_ap: e.dma_start(out=o, in_=i)], sem))
        self.ninst += 1 + len(waits)
        return ev

    def emit(self):
        nc = self.nc
        with contextlib.ExitStack() as st:
            S = {}
            for e in self.ENGS:
                S[e] = st.enter_context(nc.semaphore("s_" + e))
            for q in self.dma_pool:
                for s in self.dma_pool[q]:
                    S[s] = st.enter_context(nc.semaphore(s))
            block = st.enter_context(nc.Block())
            finals = {e: [] for e in self.ENGS}
            for q in self.dma_pool:
                for s in self.dma_pool[q]:
                    if self.dma_cnt[s] > 0:
                        finals[q].append((s, 16 * self.dma_cnt[s]))

            def run(eng_name, eobj):
                for kind, waits, fns, sem in self.streams[eng_name]:
                    for k, v in waits:
                        eobj.wait_ge(S[k], v)
                    ins = None
                    for f in fns:
                        ins = f(eobj)
                    if kind == "op":
                        ins.then_inc(S[eng_name], 1)
                    else:
                        ins.then_inc(S[sem], 16)
                for k, v in finals[eng_name]:
                    eobj.wait_ge(S[k], v)

            @block.tensor
            def _(t):
                run("pe", t)

            @block.scalar
            def _(a):
                run("act", a)

            @block.vector
            def _(v):
                run("dve", v)

            @block.gpsimd
            def _(g):
                run("pool", g)

            @block.sync
            def _(s):
                run("sp", s)


def I_act(out, in_, func, **kw):
    return lambda e: e.activation(out=out, in_=in_, func=func, **kw)


def I_tt(out, in0, in1, op):
    return lambda e: e.tensor_tensor(out=out, in0=in0, in1=in1, op=op)


def I_ts(out, in0, s1, s2, op0, op1=None):
    if op1 is None:
        return lambda e: e.tensor_scalar(out=out, in0=in0, scalar1=s1, scalar2=None, op0=op0)
    return lambda e: e.tensor_scalar(out=out, in0=in0, scalar1=s1, scalar2=s2, op0=op0, op1=op1)


def I_stt(out, in0, scalar, in1, op0, op1):
    return lambda e: e.scalar_tensor_tensor(out=out, in0=in0, scalar=scalar, in1=in1, op0=op0, op1=op1)


def I_copy(out, in_):
    return lambda e: e.tensor_copy(out=out, in_=in_)


def I_memset(ap, val):
    return lambda e: e.memset(ap, val)


def I_mm(out, lhsT, rhs, start, stop, skip=False):
    if skip:
        return lambda e: e.matmul(out, lhsT=lhsT, rhs=rhs, start=start, stop=stop, skip_group_check=True)
    return lambda e: e.matmul(out, lhsT=lhsT, rhs=rhs, start=start, stop=stop)


def I_tr(out, in_, ident):
    return lambda e: e.transpose(out=out, in_=in_, identity=ident)


def I_scan(out, data):
    return lambda e: e.tensor_tensor_scan(out=out, data0=data, data1=data, initial=0.0,
                                          op0=ALU.add, op1=ALU.bypass)


class HState:
    def __init__(self, S32, sbfs):
        self.S32 = S32
        self.sbfs = sbfs
        self.i = 0

    def cur(self):
        return self.sbfs[self.i % 2]

    def nxt(self):
        self.i += 1
        return self.sbfs[self.i % 2]


def build_program(SEQ, NP=2, NS=2, NR=5, do_sample=True, KS=31, KD=23):
    assert SEQ % 512 == 0
    nc = bass.Bass("TRN2", target_bir_lowering=False)

    def din(name, shape):
        return nc.dram_tensor(name, shape, F32, kind="ExternalInput").ap()

    def dout(name, shape):
        return nc.dram_tensor(name, shape, F32, kind="ExternalOutput").ap()

    xp = din("xp", [NP, SEQ, D])
    xs = din("xs", [NS, 64, D])
    sconv = din("sconv", [NS, 30, DC])
    shg = din("shg", [NS, 4, 128, 128])
    W = {}
    for f in (1, 2):
        W["w1", f] = din("ffn%d_w1" % f, [D, DFF])
        W["w3", f] = din("ffn%d_w3" % f, [D, DFF])
        W["w2", f] = din("ffn%d_w2" % f, [DFF, D])
    w_in = din("w_in", [D, 3072])
    w_out = din("w_out", [D, D])
    gains = {k: din(k, [D]) for k in ("ffn1_norm", "mix_norm", "ffn2_norm", "final_norm")}
    dw_w = din("conv_dw_w", [31, DC])
    dw_b = din("conv_dw_b", [DC])
    ln_g = din("conv_ln_g", [DC])
    ln_b = din("conv_ln_b", [DC])
    lbl = din("hg_lb_logits", [2, DC])
    gn = din("hg_gnorm", [DC])

    yp = dout("yp", [NP, SEQ, D])
    ys = dout("ys", [NS, 64, D])
    ncp = dout("ncp", [NP, 30, DC])
    nhp = dout("nhp", [NP, 4, 128, 128])
    ncs = dout("ncs", [NS, 30, DC])
    nhs = dout("nhs", [NS, 4, 128, 128])

    ARENA = 207 * 1024
    ctx = Ctx(nc, ARENA, {"sp": 24, "pool": 24, "act": 4})
    banks = []
    for i in range(8):
        t = nc.alloc_psum_tensor("bank%d" % i, [128, 512], F32)
        banks.append((t, Cell()))
    bank_i = [0]

    def nb():
        b = banks[bank_i[0] % 8]
        bank_i[0] += 1
        return b

    slabs = {}

    def new_slab(key, a, b):
        slabs[key] = (len(slabs), a, b, Cell())

    for f in (1, 2):
        for g in range(6):
            nch = 4 if g < 5 else 2
            new_slab(("w1", f, g), 8, nch * 128)
            new_slab(("w3", f, g), 8, nch * 128)
        for g in range(6):
            njc = 4 if g < 5 else 2
            new_slab(("w2", f, g), njc, 1024)
    for s in range(6):
        new_slab(("win", s), 8, 512)
    for h in range(2):
        new_slab(("wout", h), 8, 512)
    for m in range(4):
        new_slab(("cv", m), 31, 128)
    NSLAB = len(slabs)
    wsc = nc.dram_tensor("wsc", [NSLAB, 128, 4096], BF16, kind="Internal").ap()

    def slab_dram(key):
        idx, a, b, _ = slabs[key]
        return wsc[idx][:, 0:a * b].rearrange("p (a b) -> p a b", b=b)

    A0, G0, Q0, F0, I0, GG0 = 0, 512, 1024, 1536, 2048, 2560
    win_cols = {
        0: [A0, A0 + 128, G0, G0 + 128],
        1: [A0 + 256, A0 + 384, G0 + 256, G0 + 384],
        2: [Q0, Q0 + 128, F0, F0 + 128],
        3: [Q0 + 256, Q0 + 384, F0 + 256, F0 + 384],
        4: [I0, I0 + 128, I0 + 256, I0 + 384],
        5: [GG0, GG0 + 128, GG0 + 256, GG0 + 384],
    }

    def fp32_pieces(key):
        idx, a, b, cell = slabs[key]
        kind = key[0]
        if kind in ("w1", "w3"):
            _, f, g = key
            return [(0, b, W[kind, f].rearrange("(kc p) n -> p kc n", p=128)[:, :, g * 512:g * 512 + b])]
        if kind == "w2":
            _, f, g = key
            return [(0, b, W["w2", f][g * 512:g * 512 + a * 128, :].rearrange("(jc p) n -> p jc n", p=128))]
        if kind == "wout":
            h = key[1]
            return [(0, b, w_out.rearrange("(kc p) n -> p kc n", p=128)[:, :, h * 512:(h + 1) * 512])]
        assert kind == "win"
        cols = win_cols[key[1]]
        srcv = w_in.rearrange("(kc p) n -> p kc n", p=128)
        out = []
        i = 0
        while i < 4:
            j = i
            while j + 1 < 4 and cols[j + 1] == cols[j] + 128:
                j += 1
            n = (j - i + 1) * 128
            out.append((i * 128, n, srcv[:, :, cols[i]:cols[i] + n]))
            i = j + 1
        return out

    converted = set()

    al = ctx.alloc
    ones = al("ones", [128, 128], F32)
    ident = al("ident", [128, 128], F32)
    identb = al("identb", [128, 128], BF16)
    mask4 = al("mask4", [128, 4, 128], F32)
    ones_dv = al("ones_dv", [128, 128], F32)
    ones_c = al("ones_c", [128, 128], F32)
    epsc = al("epsc", [128, 8], F32)
    cw = al("cw", [128, 124], F32)
    pv = al("pv", [128, 24], F32)
    lbp = al("lbp", [128, 16], F32)
    stg1 = al("stg1", [128, 128], F32)
    stg2 = al("stg2", [128, 128], F32)
    ss = al("ss", [128, 8], F32)
    ms = al("ms", [128, 8], F32)
    rstd = al("rstd", [128, 8], F32)
    ebl = al("ebl", [128, 4, 8, 1], F32)
    gbc = {k: al("gbc_" + k, [128, D], F32) for k in gains}
    xts = [al("xt%d" % i, [128, 4, D], F32) for i in range(2)]
    ubuf = al("ubuf", [128, 4, 542], BF16)
    u32s = [al("u32l%d" % i, [128, 4, 32], F32) for i in range(2)]
    ubS = [al("ubS%d" % i, [128, 4, 94], BF16) for i in range(NS)]
    S32s = [al("S32_%d" % i, [128, 4, 128], F32) for i in range(2)]
    Sbfs = [[al("Sbf_%d_%d" % (i, j), [128, 4, 128], BF16) for j in range(2)] for i in range(2)]
    v_tm = al("v_tm", [128, 4, 512], BF16)
    gsn = al("gsn", [128, 4, 512], F32)
    r_fm = al("r_fm", [128, 4, 512], BF16)
    ring = [al("ring%d" % i, [128, 4096], BF16) for i in range(NR)]
    HT = ctx.ptr = (ctx.ptr + CELL - 1) // CELL * CELL
    h_tm = al("h_tm", [128, 4, D], BF16)
    qt = al("qt", [128, 4, 512], BF16, at=HT)
    kt = al("kt", [128, 4, 512], BF16, at=HT + 4096)
    HF = ctx.ptr = (ctx.ptr + CELL - 1) // CELL * CELL
    h_fm = al("h_fm", [128, 8, 512], BF16)
    kh = al("kh", [128, 4, 512], BF16, at=HF)
    c_fm = al("c_fm", [128, 4, 512], BF16, at=HF + 4096)
    MT = ctx.ptr = (ctx.ptr + CELL - 1) // CELL * CELL
    g_fm = al("g_fm", [128, NJ, 512], BF16)
    sa = Rot([al("sa%d" % i, [128, 512], F32) for i in range(2)])
    p = MT
    ah_t = []
    for i in range(3):
        ah_t.append(al("ah%d" % i, [128, 512], F32, at=p)); p += 2048
    th_t = []
    for i in range(2):
        th_t.append(al("th%d" % i, [128, 512], F32, at=p)); p += 2048
    qs = al("qs", [128, 4, 512], F32, at=p); p += 8192
    thf = al("thf", [128, 4, 512], F32, at=p); p += 8192
    kk = al("kk", [128, 4, 512], F32, at=p); p += 8192
    b_t = []
    for i in range(2):
        b_t.append(al("bb%d" % i, [128, 512], F32, at=p)); p += 2048
    E_t = []
    for i in range(3):
        E_t.append(al("E%d" % i, [128, 512], F32, at=p)); p += 2048
    D_t = []
    for i in range(2):
        D_t.append(al("Dd%d" % i, [128, 512], F32, at=p)); p += 2048
    MT_END1 = p
    p = MT
    ycv = al("ycv", [128, 4, 512], F32, at=p); p += 8192
    p += 4096
    p += 4096
    tt_t = []
    for i in range(2):
        tt_t.append(al("tt%d" % i, [128, 512], F32, at=p)); p += 2048
    Am_t = []
    for i in range(2):
        Am_t.append(al("Am%d" % i, [128, 4, 128], BF16, at=p)); p += 1024
    khT_t = []
    for i in range(2):
        khT_t.append(al("khT%d" % i, [128, 4, 128], BF16, at=p)); p += 1024
    osq_t = []
    for i in range(2):
        osq_t.append(al("osq%d" % i, [128, 512], BF16, at=p)); p += 2048
    osb_t = []
    for i in range(2):
        osb_t.append(al("osb%d" % i, [128, 512], F32, at=p)); p += 2048
    tv2 = al("tv2", [128, 512], F32, at=p); p += 2048
    rs2 = al("rs2", [128, 512], F32, at=p); p += 2048
    t1 = al("t1", [128, 512], F32, at=p); p += 2048
    MT_END2 = p
    ctx.ptr = max(ctx.ptr, MT_END1, MT_END2)
    dgs = al("dgs", [128, 31, 128], BF16)
    fjunk = al("fjunk", [128, D], BF16, at=dgs.addr)
    tv = al("tv", [128, 512], F32, at=dgs.addr + 2048)
    stgc = al("stgc", [128, 512], F32, at=dgs.addr + 2048)
    qp = al("qp", [128, 4, 256], BF16)
    ones_dvb = al("ones_dvb", [128, 128], BF16)
    ones_cb = al("ones_cb", [128, 128], BF16)
    rs = al("rs", [128, 512], F32, at=dgs.addr + 4096)
    sq_t = [al("sqA", [128, 512], BF16, at=MT + 8192), al("sqB", [128, 512], BF16)]
    b_t.append(al("bb2", [128, 512], F32))
    D_t.append(al("Dd2", [128, 512], F32))
    assert ctx.ptr <= ARENA, ctx.ptr
    tt4 = [tt_t[0], tt_t[1], osq_t[0], osq_t[1]]
    ah_r, th_r, b_r, E_r, D_r = Rot(ah_t), Rot(th_t), Rot(b_t), Rot(E_t), Rot(D_t)
    sq_r, tt_r, Am_r, khT_r, osq_r, osb_r = Rot(sq_t), Rot(tt_t), Rot(Am_t), Rot(khT_t), Rot(osq_t), Rot(osb_t)

    op = ctx.op
    op("pool", I_memset(ones[:, :], 1.0), writes=[ones])
    op("pool", lambda e: e.affine_select(out=ident[:, :], in_=ones[:, :], pattern=[[-1, 128]],
                                         compare_op=ALU.is_equal, fill=0.0, base=0, channel_multiplier=1),
       reads=[ones], writes=[ident])
    for hh in range(4):
        op("pool", lambda e, hh=hh: e.affine_select(out=mask4[:, hh, :], in_=ones[:, :], pattern=[[1, 128]],
                                                     compare_op=ALU.is_ge, fill=0.0, base=0, channel_multiplier=-1),
           reads=[ones], writes=[mask4])
    op("pool", I_memset(ones_dv[:, :], 1.0 / 128), writes=[ones_dv])
    op("pool", I_memset(ones_c[:, :], 1.0 / 512), writes=[ones_c])
    op("pool", I_memset(ones_dvb[:, :], 1.0 / 128), writes=[ones_dvb])
    op("pool", I_memset(ones_cb[:, :], 1.0 / 512), writes=[ones_cb])
    op("pool", I_memset(epsc[:, :], EPS), writes=[epsc])
    op("dve", I_copy(identb[:, :], ident[:, :]), reads=[ident], writes=[identb])
    ctx.dma("sp", stg1[0:124, :], dw_w.rearrange("k (c p) -> (k c) p", p=128), writes=[stg1])
    for r0, vec in ((0, dw_b), (4, ln_g), (8, ln_b), (12, gn)):
        ctx.dma("sp", stg2[r0:r0 + 4, :], vec.rearrange("(c p) -> c p", p=128), writes=[stg2])
    ctx.dma("sp", stg2[16:24, :], lbl.rearrange("r (c p) -> (r c) p", p=128), writes=[stg2])
    for k, g_ap in gains.items():
        ctx.dma("sp", gbc[k][:, :], g_ap.partition_broadcast(128), writes=[gbc[k]])
    b = nb()
    op("pe", I_tr(b[0][:, 0:124], stg1[0:124, :], ident[0:124, 0:124]), reads=[stg1, ident], writes=[b[1]])
    op("dve", I_copy(cw[:, :], b[0][:, 0:124]), writes=[b[1], cw])
    b = nb()
    op("pe", I_tr(b[0][:, 0:24], stg2[0:24, :], ident[0:24, 0:24]), reads=[stg2, ident], writes=[b[1]])
    op("dve", I_copy(pv[:, :], b[0][:, 0:24]), writes=[b[1], pv])
    op("dve", I_tt(lbp[:, 0:4], pv[:, 16:20], pv[:, 20:24], ALU.subtract), reads=[pv], writes=[lbp])
    op("act", I_act(lbp[:, 0:4], lbp[:, 0:4], AF.Tanh, scale=0.5), reads=[lbp], writes=[lbp])
    op("dve", I_ts(lbp[:, 4:8], lbp[:, 0:4], -0.25, 0.25, ALU.mult, ALU.add), reads=[lbp], writes=[lbp])
    op("dve", I_ts(lbp[:, 8:12], lbp[:, 0:4], 0.25, 0.75, ALU.mult, ALU.add), reads=[lbp], writes=[lbp])
    op("dve", I_ts(lbp[:, 12:16], lbp[:, 0:4], 0.25, -0.25, ALU.mult, ALU.add), reads=[lbp], writes=[lbp])

    def C1(m):
        return lbp[:, 4 + m:5 + m]

    def C0(m):
        return lbp[:, 8 + m:9 + m]

    def NC1(m):
        return lbp[:, 12 + m:13 + m]

    def build_diag(m):
        for k in range(31):
            op("act", I_act(dgs[:, k, :], identb[:, :], AF.Identity, scale=cw[:, k * 4 + m:k * 4 + m + 1]),
               reads=[identb, cw], writes=[dgs.cells(k * 128, 128)])
        ctx.dma("act", slab_dram(("cv", m)), dgs[:, :, :], reads=[dgs], writes=[slabs[("cv", m)][3]])

    ring_i = [0]

    def load_slab(key):
        idx, a, b_, cell = slabs[key]
        t = ring[ring_i[0] % NR]
        ring_i[0] += 1
        dst = t.ap[:, 0:a * b_].rearrange("p (a b) -> p a b", b=b_)
        cells = t.cells(0, a * b_)
        if key[0] != "cv" and key not in converted:
            converted.add(key)
            for (c_lo, n, src) in fp32_pieces(key):
                ctx.dma("pool", dst[:, :, c_lo:c_lo + n], src, writes=[cells])
            ctx.dma("sp", slab_dram(key), dst, reads=[cells], writes=[cell])
        else:
            ctx.dma("sp", dst, slab_dram(key), reads=[cell], writes=[cells])
        return cells, dst

    def norm_front(xt, tg, gk):
        op("dve", I_memset(ss[:, tg:tg + 1], 0.0), writes=[ss])
        op("act", I_act(h_tm[:, tg, :], xt[:, tg, :], AF.Square, accum_out=ss[:, tg:tg + 1]),
           reads=[xt.cells(tg * D, D)], writes=[h_tm.cells(tg * D, D), ss])
        op("act", I_act(ms[:, tg:tg + 1], ss[:, tg:tg + 1], AF.Ln, scale=1.0 / D, bias=epsc[:, 0:1]),
           reads=[ss, epsc], writes=[ms])
        op("act", I_act(rstd[:, tg:tg + 1], ms[:, tg:tg + 1], AF.Exp, scale=-0.5), reads=[ms], writes=[rstd])
        op("dve", I_stt(h_tm[:, tg, :], xt[:, tg, :], rstd[:, tg:tg + 1], gbc[gk][:, :], ALU.mult, ALU.mult),
           reads=[xt.cells(tg * D, D), rstd, gbc[gk]], writes=[h_tm.cells(tg * D, D)])

    def norm_back(tg, bank=None):
        bk = bank if bank is not None else nb()
        bv = bk[0][:, :].bitcast(BF16)
        op("pe", [I_tr(bv[:, fc * 128:(fc + 1) * 128], h_tm[:, tg, fc * 128:(fc + 1) * 128], identb[:, :])
                  for fc in range(8)], reads=[h_tm.cells(tg * D, D), identb], writes=[bk[1]])
        src = bv[:, :].rearrange("p (f t) -> p f t", t=128)
        dst = h_fm[:, :, tg * 128:(tg + 1) * 128]
        hc = [h_fm.cells(fc * 512 + tg * 128, 128) for fc in range(8)]
        if tg % 2 == 0:
            op("act", I_act(dst, src, AF.Copy), writes=[bk[1], hc])
        else:
            op("dve", I_copy(dst, src), writes=[bk[1], hc])

    def norm_stage(xt, NG, gk):
        for tg in range(NG):
            norm_front(xt, tg, gk)
        for tg in range(NG):
            norm_back(tg)

    def ffn_up(f, NG, after_group=None, split=False, mid_hook=None):
        ntok = NG * 128
        for g in range(6):
            nch = 4 if g < 5 else 2
            c1_, v1 = load_slab(("w1", f, g))
            c3_, v3 = load_slab(("w3", f, g))
            if g == 0 and not (split and NG == 4) and mid_hook is not None:
                mid_hook()
            if split and g == 0 and NG == 4:
                nsp = 3
                bks = [(nb(), nb()) for i in range(nsp)]
                for (lo, n) in ((0, 384), (384, 128)):
                    if lo > 0 and mid_hook is not None:
                        mid_hook()
                    hc = [h_fm.cells(kc * 512 + lo, n) for kc in range(8)]
                    for i in range(nsp):
                        bA, bB = bks[i]
                        op("pe", [I_mm(bA[0][:, lo:lo + n], v1[:, kc, i * 128:(i + 1) * 128], h_fm[:, kc, lo:lo + n],
                                       kc == 0, kc == 7) for kc in range(8)], reads=[c1_, hc], writes=[bA[1]])
                        op("pe", [I_mm(bB[0][:, lo:lo + n], v3[:, kc, i * 128:(i + 1) * 128], h_fm[:, kc, lo:lo + n],
                                       kc == 0, kc == 7) for kc in range(8)], reads=[c3_, hc], writes=[bB[1]])
                for i in range(nch):
                    j = g * 4 + i
                    if i < nsp:
                        bA, bB = bks[i]
                    else:
                        bA = nb()
                        bB = nb()
                        op("pe", [I_mm(bA[0][:, 0:ntok], v1[:, kc, i * 128:(i + 1) * 128], h_fm[:, kc, 0:ntok], kc == 0, kc == 7)
                                  for kc in range(8)], reads=[c1_, h_fm], writes=[bA[1]])
                        op("pe", [I_mm(bB[0][:, 0:ntok], v3[:, kc, i * 128:(i + 1) * 128], h_fm[:, kc, 0:ntok], kc == 0, kc == 7)
                                  for kc in range(8)], reads=[c3_, h_fm], writes=[bB[1]])
                    s = sa.next()
                    op("act", I_act(s[:, 0:ntok], bA[0][:, 0:ntok], AF.Silu), writes=[bA[1], s])
                    op("dve", I_tt(g_fm[:, j, 0:ntok], s[:, 0:ntok], bB[0][:, 0:ntok], ALU.mult),
                       reads=[s], writes=[bB[1], g_fm.cells(j * 512, ntok)])
                if after_group is not None:
                    after_group(g)
                continue
            for i in range(nch):
                j = g * 4 + i
                bA = nb()
                bB = nb()
                op("pe", [I_mm(bA[0][:, 0:ntok], v1[:, kc, i * 128:(i + 1) * 128], h_fm[:, kc, 0:ntok], kc == 0, kc == 7)
                          for kc in range(8)], reads=[c1_, h_fm], writes=[bA[1]])
                op("pe", [I_mm(bB[0][:, 0:ntok], v3[:, kc, i * 128:(i + 1) * 128], h_fm[:, kc, 0:ntok], kc == 0, kc == 7)
                          for kc in range(8)], reads=[c3_, h_fm], writes=[bB[1]])
                s = sa.next()
                op("act", I_act(s[:, 0:ntok], bA[0][:, 0:ntok], AF.Silu), writes=[bA[1], s])
                op("dve", I_tt(g_fm[:, j, 0:ntok], s[:, 0:ntok], bB[0][:, 0:ntok], ALU.mult),
                   reads=[s], writes=[bB[1], g_fm.cells(j * 512, ntok)])
            if after_group is not None:
                after_group(g)

    def ffn_down(f, xt, NG, after_mm=None, after_pass=None):
        passes = [[0, 1, 2], [3]] if NG == 4 else [list(range(NG))]
        for pp, tgs in enumerate(passes):
            accs = {}
            for tg in tgs:
                for ch in range(2):
                    accs[(tg, ch)] = nb()
            for g in range(6):
                njc = 4 if g < 5 else 2
                c2_, v2 = load_slab(("w2", f, g))
                fns = []
                for jc in range(njc):
                    j = g * 4 + jc
                    for tg in tgs:
                        for ch in range(2):
                            fns.append(I_mm(accs[(tg, ch)][0][:, :], g_fm[:, j, tg * 128:(tg + 1) * 128],
                                            v2[:, jc, ch * 512:(ch + 1) * 512], j == 0, j == NJ - 1))
                op("pe", fns, reads=[c2_, g_fm], writes=[a[1] for a in accs.values()])
            for (tg, ch), a in accs.items():
                xv = xt[:, tg, ch * 512:(ch + 1) * 512]
                op("dve", I_stt(xv, a[0][:, :], 0.5, xv, ALU.mult, ALU.add),
                   reads=[xt.cells(tg * D + ch * 512, 512)], writes=[a[1], xt.cells(tg * D + ch * 512, 512)])
            if after_mm is not None:
                after_mm(pp)
            if after_pass is not None:
                after_pass(pp, tgs)

    def mixer(xt, NG, segs, hst, last, out_conv, out_hg, after_tg=None, before_res=None, first_hook=None):
        ntok = NG * 128
        nchunk = ntok // 64
        for sl in range(2):
            cs, v = load_slab(("win", sl))
            ahs = []
            pre_bks = None
            if sl == 0 and NG != 4 and first_hook is not None:
                first_hook()
            if sl == 0 and NG == 4:
                pre_bks = [nb() for i in range(4)]
                for (lo, n) in ((0, 384), (384, 128)):
                    if lo > 0 and first_hook is not None:
                        first_hook()
                    hc = [h_fm.cells(kc * 512 + lo, n) for kc in range(8)]
                    for i in range(4):
                        bk = pre_bks[i]
                        op("pe", [I_mm(bk[0][:, lo:lo + n], v[:, kc, i * 128:(i + 1) * 128], h_fm[:, kc, lo:lo + n],
                                       kc == 0, kc == 7) for kc in range(8)], reads=[cs, hc], writes=[bk[1]])
            for i in range(4):
                if pre_bks is not None:
                    bk = pre_bks[i]
                else:
                    bk = nb()
                    op("pe", [I_mm(bk[0][:, 0:ntok], v[:, kc, i * 128:(i + 1) * 128], h_fm[:, kc, 0:ntok], kc == 0, kc == 7)
                              for kc in range(8)], reads=[cs, h_fm], writes=[bk[1]])
                if i < 2:
                    a = ah_r.next()
                    ahs.append(a)
                    op("act", I_act(a[:, 0:ntok], bk[0][:, 0:ntok], AF.Identity, scale=0.5), writes=[bk[1], a])
                else:
                    m = 2 * sl + (i - 2)
                    h = th_r.next()
                    op("act", I_act(h[:, 0:ntok], bk[0][:, 0:ntok], AF.Tanh, scale=0.5), writes=[bk[1], h])
                    for si, (ub, Wd, c0, L) in enumerate(segs):
                        op("dve", I_stt(ub[:, m, 30:30 + L], h[:, c0:c0 + L], 1.0, ahs[i - 2][:, c0:c0 + L], ALU.add, ALU.mult),
                           reads=[h, ahs[i - 2]], writes=[ub.cells(m * Wd + 30, L)])
                        if last:
                            e0 = c0 + L - 30
                            op("dve", I_stt(u32s[si][:, m, 0:30], h[:, e0:e0 + 30], 1.0, ahs[i - 2][:, e0:e0 + 30], ALU.add, ALU.mult),
                               reads=[h, ahs[i - 2]], writes=[u32s[si]])
        dve_bg = []
        for (ub, Wd, c0, L) in segs:
            for k in range(KD, 31):
                for m in range(4):
                    yv = ycv[:, m, c0:c0 + L]
                    if k == KD:
                        dve_bg.append((I_ts(yv, ub[:, m, k:k + L], cw[:, k * 4 + m:k * 4 + m + 1], None, ALU.mult),
                                       [ub.cells(m * Wd, Wd), cw], [ycv.cells(m * 512 + c0, L)]))
                    else:
                        dve_bg.append((I_stt(yv, ub[:, m, k:k + L], cw[:, k * 4 + m:k * 4 + m + 1], yv, ALU.mult, ALU.add),
                                       [ub.cells(m * Wd, Wd), ycv.cells(m * 512 + c0, L), cw], [ycv.cells(m * 512 + c0, L)]))

        def bgd(n):
            for _ in range(n):
                if dve_bg:
                    f_, r_, w_ = dve_bg.pop(0)
                    op("dve", f_, reads=r_, writes=w_)

        for sl in (2, 3):
            cs, v = load_slab(("win", sl))
            for i in range(4):
                m = 2 * (sl - 2) + i % 2
                bk = nb()
                op("pe", [I_mm(bk[0][:, 0:ntok], v[:, kc, i * 128:(i + 1) * 128], h_fm[:, kc, 0:ntok], kc == 0, kc == 7)
                          for kc in range(8)], reads=[cs, h_fm], writes=[bk[1]])
                if i < 2:
                    op("act", I_act(qs[:, m, 0:ntok], bk[0][:, 0:ntok], AF.Silu), writes=[bk[1], qs.cells(m * 512, ntok)])
                else:
                    op("act", I_act(thf[:, m, 0:ntok], bk[0][:, 0:ntok], AF.Tanh, scale=0.5),
                       writes=[bk[1], thf.cells(m * 512, ntok)])
                    op("dve", I_ts(kk[:, m, 0:ntok], thf[:, m, 0:ntok], NC1(m), C1(m), ALU.mult, ALU.add),
                       reads=[thf.cells(m * 512, ntok), lbp], writes=[kk.cells(m * 512, ntok)])
                bgd(4)
        cs, v = load_slab(("win", 5))
        for m in range(4):
            bk = nb()
            op("pe", [I_mm(bk[0][:, 0:ntok], v[:, kc, m * 128:(m + 1) * 128], h_fm[:, kc, 0:ntok], kc == 0, kc == 7)
                      for kc in range(8)], reads=[cs, h_fm], writes=[bk[1]])
            op("act", I_act(gsn[:, m, 0:ntok], bk[0][:, 0:ntok], AF.Silu), writes=[bk[1], gsn.cells(m * 512, ntok)])
            op("dve", I_ts(gsn[:, m, 0:ntok], gsn[:, m, 0:ntok], pv[:, 12 + m:13 + m], None, ALU.mult),
               reads=[gsn.cells(m * 512, ntok), pv], writes=[gsn.cells(m * 512, ntok)])
            bgd(2)
        cs, v = load_slab(("win", 4))
        for tg in range(NG):
            bk = nb()
            op("pe", [I_mm(bk[0][:, :], h_fm[:, kc, tg * 128:(tg + 1) * 128], v[:, kc, :], kc == 0, kc == 7)
                      for kc in range(8)], reads=[cs, h_fm], writes=[bk[1]])
            op("act", I_act(v_tm[:, tg, :], bk[0][:, :], AF.Copy), writes=[bk[1], v_tm.cells(tg * 512, 512)])
            bgd(2)
        bgd(10 ** 6)
        for m in range(4):
            fm = thf[:, m, 0:ntok]
            fcell = thf.cells(m * 512, ntok)
            op("act", I_act(fm, fm, AF.Ln, scale=C1(m), bias=C0(m)), reads=[fcell, lbp], writes=[fcell])
        bbs, dds = {}, {}

        def prep_scan(m):
            fcell = thf.cells(m * 512, ntok)
            bb = b_r.next()
            bbs[m] = bb
            for c in range(nchunk):
                op("dve", I_scan(bb[:, c * 64:(c + 1) * 64], thf[:, m, c * 64:(c + 1) * 64]),
                   reads=[fcell], writes=[bb.cells(c * 64, 64)])
            bb3 = bb[:, 0:ntok].rearrange("p (c j) -> p c j", j=64)
            dd = D_r.next()
            dds[m] = dd
            dd3 = dd[:, 0:ntok].rearrange("p (c j) -> p c j", j=64)
            op("dve", I_tt(dd3, bb3[:, :, 63:64].to_broadcast([128, nchunk, 64]), bb3, ALU.subtract),
               reads=[bb], writes=[dd])

        def prep_exp(m):
            bb, dd = bbs[m], dds[m]
            bb3 = bb[:, 0:ntok].rearrange("p (c j) -> p c j", j=64)
            e1 = E_r.next()
            op("act", I_act(e1[:, 0:ntok], bb[:, 0:ntok], AF.Exp), reads=[bb], writes=[e1])
            e2 = E_r.next()
            op("act", I_act(e2[:, 0:ntok], bb[:, 0:ntok], AF.Exp, scale=-1.0), reads=[bb], writes=[e2])
            e3 = E_r.next()
            op("act", I_act(e3[:, 0:ntok], dd[:, 0:ntok], AF.Exp), reads=[dd], writes=[e3])
            op("act", I_act(ebl[:, m, 0:nchunk, :], bb3[:, :, 63:64], AF.Exp), reads=[bb], writes=[ebl])
            op("dve", I_tt(qt[:, m, 0:ntok], qs[:, m, 0:ntok], e1[:, 0:ntok], ALU.mult),
               reads=[qs.cells(m * 512, ntok), e1], writes=[qt.cells(m * 512, ntok)])
            op("dve", I_tt(kt[:, m, 0:ntok], kk[:, m, 0:ntok], e2[:, 0:ntok], ALU.mult),
               reads=[kk.cells(m * 512, ntok), e2], writes=[kt.cells(m * 512, ntok)])
            op("dve", I_tt(kh[:, m, 0:ntok], kk[:, m, 0:ntok], e3[:, 0:ntok], ALU.mult),
               reads=[kk.cells(m * 512, ntok), e3], writes=[kh.cells(m * 512, ntok)])
            npair = ntok // 128
            q4 = qt[:, m, 0:ntok].rearrange("p (a two j) -> p a two j", two=2, j=64)
            e4 = ebl[:, m, 0:nchunk, :].rearrange("p (a two) o -> p a two o", two=2)
            op("dve", I_tt(qp[:, m, 0:npair * 64].rearrange("p (a j) -> p a j", j=64), q4[:, :, 1, :],
                           e4[:, :, 0, :].to_broadcast([128, npair, 64]), ALU.mult),
               reads=[qt.cells(m * 512, ntok), ebl], writes=[qp.cells(m * 256, npair * 64)])

        prep_scan(0)
        prep_scan(1)
        prep_exp(0)
        prep_scan(2)
        prep_exp(1)
        prep_scan(3)
        prep_exp(2)
        prep_exp(3)
        for m in range(4):
            cs, v = load_slab(("cv", m))
            bk = nb()
            fns = []
            first = True
            for (ub, Wd, c0, L) in segs:
                for k in range(KD):
                    fns.append(I_mm(bk[0][:, c0:c0 + L], v[:, k, :], ub[:, m, k:k + L], first,
                                    (KD == 31 and k == KD - 1 and c0 + L == ntok), skip=True))
                    first = False
            rds = [cs] + [ub.cells(m * Wd, Wd) for (ub, Wd, c0, L) in segs]
            if KD < 31:
                fns.append(I_mm(bk[0][:, 0:ntok], ident[:, :], ycv[:, m, 0:ntok], False, True, skip=True))
                rds += [ident, ycv.cells(m * 512, ntok)]
            op("pe", fns, reads=rds, writes=[bk[1]])
            op("act", I_act(ycv[:, m, 0:ntok], bk[0][:, 0:ntok], AF.Identity, bias=pv[:, m:m + 1]),
               reads=[pv], writes=[bk[1], ycv.cells(m * 512, ntok)])
        if last:
            pass

        def bg(n):
            return

        for si, (ub, Wd, c0, L) in enumerate(segs):
            if not last:
                op("pool", I_copy(ub[:, :, 0:30], ub[:, :, L:L + 30]), reads=[ub], writes=[ub])
            else:
                bk = nb()
                op("pe", [I_tr(bk[0][0:30, m * 128:(m + 1) * 128], u32s[si][:, m, 0:30], ident[:, :]) for m in range(4)],
                   reads=[u32s[si], ident], writes=[bk[1]])
                op("act", I_act(stgc[0:30, :], bk[0][0:30, :], AF.Copy), writes=[bk[1], stgc])
                ctx.dma("pool", out_conv[si], stgc[0:30, :], reads=[stgc])
        def ln_stage(k):
            if k == 0:
                bm = banks[6]
                for m in range(4):
                    op("pe", I_mm(bm[0][:, 0:ntok], ones_c[:, :], ycv[:, m, 0:ntok], m == 0, m == 3),
                       reads=[ones_c, ycv.cells(m * 512, ntok)], writes=[bm[1]])
                for m in range(4):
                    yv = ycv[:, m, 0:ntok]
                    op("dve", I_tt(yv, yv, bm[0][:, 0:ntok], ALU.subtract),
                       reads=[ycv.cells(m * 512, ntok)], writes=[bm[1], ycv.cells(m * 512, ntok)])
            elif k == 1:
                bvv = banks[7]
                for m in range(4):
                    sq = sq_r.next()
                    op("act", I_act(sq[:, 0:ntok], ycv[:, m, 0:ntok], AF.Square), reads=[ycv.cells(m * 512, ntok)], writes=[sq])
                    op("pe", I_mm(bvv[0][:, 0:ntok], ones_cb[:, :], sq[:, 0:ntok], m == 0, m == 3),
                       reads=[ones_cb, sq], writes=[bvv[1]])
                op("act", I_act(tv[:, 0:ntok], bvv[0][:, 0:ntok], AF.Ln, bias=epsc[:, 0:1]), reads=[epsc], writes=[bvv[1], tv])
                op("act", I_act(rs[:, 0:ntok], tv[:, 0:ntok], AF.Exp, scale=-0.5), reads=[tv], writes=[rs])
            elif k == 2:
                for m in range(4):
                    yv = ycv[:, m, 0:ntok]
                    op("dve", I_tt(yv, yv, rs[:, 0:ntok], ALU.mult),
                       reads=[ycv.cells(m * 512, ntok), rs], writes=[ycv.cells(m * 512, ntok)])
            else:
                for m in range(4):
                    op("act", I_act(c_fm[:, m, 0:ntok], ycv[:, m, 0:ntok], AF.Silu, scale=pv[:, 4 + m:5 + m],
                                    bias=pv[:, 8 + m:9 + m]),
                       reads=[ycv.cells(m * 512, ntok), pv], writes=[c_fm.cells(m * 512, ntok)])

        def hg_front(pr):
            c0 = pr * 128
            paired = hst[2 * pr] is hst[2 * pr + 1]
            bA = banks[0]
            fns = []
            for hh in range(4):
                fns.append(I_mm(bA[0][:, hh * 128:(hh + 1) * 128], kt[:, hh, c0:c0 + 128], qt[:, hh, c0:c0 + 128], True, True))
                if paired:
                    fns.append(I_mm(bA[0][0:64, hh * 128 + 64:(hh + 1) * 128], kh[:, hh, c0:c0 + 64],
                                    qt[:, hh, c0 + 64:c0 + 128], True, True))
            op("pe", fns, reads=[kt, qt, kh], writes=[bA[1]])
            am = Am_r.next()
            op("dve", I_tt(am[:, :, :], bA[0][:, :].rearrange("p (h t) -> p h t", t=128), mask4[:, :, :], ALU.mult),
               reads=[mask4], writes=[bA[1], am])
            if not paired:
                op("dve", I_memset(am[0:64, :, 64:128], 0.0), writes=[am])
            bT = banks[1]
            bTv = bT[0][:, :].bitcast(BF16)
            op("pe", [I_tr(bTv[:, hh * 128:(hh + 1) * 128], kh[:, hh, c0:c0 + 128], identb[:, :]) for hh in range(4)],
               reads=[kh, identb], writes=[bT[1]])
            kT = khT_r.next()
            op("dve", I_copy(kT[:, :, :], bTv[:, 0:512].rearrange("p (h t) -> p h t", t=128)),
               writes=[bT[1], kT])
            return am, kT

        def hg_chain(pr, am, kT):
            c0 = pr * 128
            st0, st1 = hst[2 * pr], hst[2 * pr + 1]
            paired = st0 is st1
            bO = banks[2 + pr % 2]
            op("pe", [I_mm(bO[0][:, hh * 128:(hh + 1) * 128], v_tm[:, pr, hh * 128:(hh + 1) * 128], am[:, hh, :],
                           hh == 0, False, skip=True) for hh in range(4)],
               reads=[v_tm.cells(pr * 512, 512), am], writes=[bO[1]])
            for c in range(2):
                b_ = banks[4 + c]
                op("pe", [I_mm(b_[0][:, hh * 128:(hh + 1) * 128], kT[c * 64:(c + 1) * 64, hh, :],
                               v_tm[c * 64:(c + 1) * 64, pr, hh * 128:(hh + 1) * 128], True, True)
                          for hh in range(4)],
                   reads=[kT, v_tm.cells(pr * 512, 512)], writes=[b_[1]])
            sb0 = st0.sbfs[st0.i % 2]
            sb1 = st1.sbfs[st1.i % 2]
            fns = []
            for hh in range(4):
                fns.append(I_mm(bO[0][:, hh * 128:hh * 128 + 64], sb0[:, hh, :], qt[:, hh, c0:c0 + 64], False, False, skip=True))
                if paired:
                    fns.append(I_mm(bO[0][:, hh * 128 + 64:(hh + 1) * 128], sb0[:, hh, :], qp[:, hh, pr * 64:(pr + 1) * 64],
                                    False, hh == 3, skip=True))
                else:
                    fns.append(I_mm(bO[0][:, hh * 128 + 64:(hh + 1) * 128], sb1[:, hh, :], qt[:, hh, c0 + 64:c0 + 128],
                                    False, hh == 3, skip=True))
            op("pe", fns, reads=[sb0, sb1, qt, qp], writes=[bO[1]])
            for half in range(2):
                for hh in (2 * half, 2 * half + 1):
                    for c in range(2):
                        st = st0 if c == 0 else st1
                        gc = pr * 2 + c
                        b_ = banks[4 + c]
                        sv = st.S32[:, hh, :]
                        op("dve", I_stt(sv, sv, ebl[:, hh, gc, :], b_[0][:, hh * 128:(hh + 1) * 128], ALU.mult, ALU.add),
                           reads=[st.S32.cells(hh * 128, 128), ebl], writes=[b_[1], st.S32.cells(hh * 128, 128)])
                for st in ([st0] if paired else [st0, st1]):
                    nsb = st.sbfs[(st.i + 1) % 2]
                    h0 = 2 * half
                    op("act", I_act(nsb[:, h0:h0 + 2, :], st.S32[:, h0:h0 + 2, :], AF.Copy),
                       reads=[st.S32.cells(h0 * 128, 256)], writes=[nsb.cells(h0 * 128, 256)])
            st0.i += 1
            if not paired:
                st1.i += 1

        def hg_tail_a(pr):
            bO = banks[2 + pr % 2]
            osq = osq_r.next()
            osb = osb_r.next()
            op("act", I_act(osq[:, :], bO[0][:, :], AF.Square), writes=[bO[1], osq])
            op("act", I_act(osb[:, :], bO[0][:, :], AF.Copy), writes=[bO[1], osb])
            return osq, osb

        def hg_tail(pr, pre=None):
            c0 = pr * 128
            osq, osb = pre if pre is not None else hg_tail_a(pr)
            bV = banks[6]
            op("pe", I_mm(bV[0][:, :], ones_dvb[:, :], osq[:, :], True, True), reads=[ones_dvb, osq], writes=[bV[1]])
            op("act", I_act(tv2[:, :], bV[0][:, :], AF.Ln, bias=epsc[:, 0:1]), reads=[epsc], writes=[bV[1], tv2])
            op("act", I_act(rs2[:, :], tv2[:, :], AF.Exp, scale=-0.5), reads=[tv2], writes=[rs2])
            op("dve", I_tt(t1[:, :], osb[:, :], rs2[:, :], ALU.mult), reads=[osb, rs2], writes=[t1])
            op("dve", I_tt(r_fm[:, :, c0:c0 + 128], t1[:, :].rearrange("p (h t) -> p h t", t=128),
                           gsn[:, :, c0:c0 + 128], ALU.mult), reads=[t1, gsn], writes=[r_fm])

        fr = hg_front(0)
        pre_prev = None
        for pr in range(NG):
            fr_next = hg_front(pr + 1) if pr + 1 < NG else None
            hg_chain(pr, fr[0], fr[1])
            if NG == 4:
                if pr == 0:
                    ln_stage(0)
                    ln_stage(1)
                elif pr == 1:
                    ln_stage(2)
                    ln_stage(3)
                if 0 < pr < NG - 1:
                    hg_tail(pr - 1)
                elif pr == NG - 1:
                    pre_prev = hg_tail_a(pr - 1)
            else:
                for k in range(4):
                    ln_stage(k)
            fr = fr_next
        if last:
            done = []
            for gc in range(nchunk):
                st = hst[gc]
                if id(st) in done:
                    continue
                done.append(id(st))
                ctx.dma("pool", out_hg[len(done) - 1].rearrange("h k v -> k h v"), st.S32[:, :, :], reads=[st.S32])
        wo = [load_slab(("wout", h)) for h in range(2)]

        def wout_mm(tg, bank_pair=None):
            bks = []
            for h in range(2):
                cs, v = wo[h]
                bk = bank_pair[h] if bank_pair is not None else nb()
                bks.append(bk)
                fns = []
                for kc in range(8):
                    src = c_fm if kc < 4 else r_fm
                    fns.append(I_mm(bk[0][:, :], src[:, kc % 4, tg * 128:(tg + 1) * 128], v[:, kc, :], kc == 0, kc == 7))
                op("pe", fns, reads=[cs] + [c_fm.cells(m * 512 + tg * 128, 128) for m in range(4)]
                   + [r_fm.cells(m * 512 + tg * 128, 128) for m in range(4)], writes=[bk[1]])
            return bks

        def wout_res(tg, bks):
            for h in range(2):
                bk = bks[h]
                xv = xt[:, tg, h * 512:(h + 1) * 512]
                op("dve", I_tt(xv, xv, bk[0][:, :], ALU.add),
                   reads=[xt.cells(tg * D + h * 512, 512)], writes=[bk[1], xt.cells(tg * D + h * 512, 512)])
            if after_tg is not None:
                after_tg(tg)

        pre_last = hg_tail_a(NG - 1)
        if NG == 4:
            b0 = wout_mm(0, (banks[0], banks[1]))
            b1 = wout_mm(1, (banks[2], banks[3]))
            hg_tail(2, pre_prev)
            hg_tail(3, pre_last)
            wout_res(0, b0)
            wout_res(1, b1)
            b2 = wout_mm(2, (banks[4], banks[5]))
            if before_res is not None:
                before_res([banks[7], banks[6]])
            wout_res(2, b2)
            b3 = wout_mm(3, (banks[0], banks[1]))
            if before_res is not None:
                before_res([banks[7]])
            wout_res(3, b3)
        else:
            for tg in range(NG):
                if tg == NG - 1:
                    hg_tail(NG - 1, pre_last)
                bks = wout_mm(tg)
                if before_res is not None:
                    before_res()
                wout_res(tg, bks)

    def final_tg(xt, tg):
        op("dve", I_memset(ss[:, 4 + tg:5 + tg], 0.0), writes=[ss])
        op("act", I_act(fjunk[:, :], xt[:, tg, :], AF.Square, accum_out=ss[:, 4 + tg:5 + tg]),
           reads=[xt.cells(tg * D, D)], writes=[fjunk, ss])
        op("act", I_act(ms[:, 4 + tg:5 + tg], ss[:, 4 + tg:5 + tg], AF.Ln, scale=1.0 / D, bias=epsc[:, 0:1]),
           reads=[ss, epsc], writes=[ms])
        op("act", I_act(rstd[:, 4 + tg:5 + tg], ms[:, 4 + tg:5 + tg], AF.Exp, scale=-0.5), reads=[ms], writes=[rstd])
        xv = xt[:, tg, :]
        op("dve", I_stt(xv, xv, rstd[:, 4 + tg:5 + tg], gbc["final_norm"][:, :], ALU.mult, ALU.mult),
           reads=[xt.cells(tg * D, D), rstd, gbc["final_norm"]], writes=[xt.cells(tg * D, D)])

    tiles = []
    nt_seq = SEQ // 512
    for b_ in range(NP):
        for ti in range(nt_seq):
            tiles.append(("p", b_, ti))
    if do_sample:
        tiles.append(("s", 0, 0))

    def load_x(idx):
        kind, b_, ti = tiles[idx]
        xt = xts[idx % 2]
        if kind == "p":
            src = xp[b_, ti * 512:(ti + 1) * 512, :].rearrange("(tg p) d -> p tg d", p=128)
            ctx.dma("sp", xt[:, 0:4, :], src, writes=[xt])
        else:
            src = xs.rearrange("s t d -> (s t) d")
            ctx.dma("sp", xt[:, 0, :], src, writes=[xt.cells(0, D)])

    hs_main = HState(S32s[0], Sbfs[0])
    load_x(0)
    for idx, (kind, b_, ti) in enumerate(tiles):
        xt = xts[idx % 2]
        if kind == "p":
            NG = 4
            if ti == 0:
                op("pool", I_memset(ubuf[:, :, 0:30], 0.0), writes=[ubuf])
                op("pool", I_memset(S32s[0][:, :, :], 0.0), writes=[S32s[0]])
                op("pool", I_memset(hs_main.cur()[:, :, :], 0.0), writes=[hs_main.cur()])
            segs = [(ubuf, 542, 0, 512)]
            hst = [hs_main] * 8
            last = (ti == nt_seq - 1)
            out_conv = [ncp[b_]]
            out_hg = [nhp[b_]]
            out_y = yp[b_, ti * 512:(ti + 1) * 512, :].rearrange("(tg p) d -> p tg d", p=128)
        else:
            NG = 1
            hst = []
            for s in range(NS):
                ctx.dma("sp", stgc[0:30, :], sconv[s], writes=[stgc])
                bk = nb()
                op("pe", [I_tr(bk[0][:, m * 32:m * 32 + 30], stgc[0:30, m * 128:(m + 1) * 128], ident[0:30, 0:30])
                          for m in range(4)], reads=[stgc, ident], writes=[bk[1]])
                op("act", I_act(ubS[s][:, :, 0:30], bk[0][:, 0:128].rearrange("p (m t) -> p m t", t=32)[:, :, 0:30], AF.Copy),
                   writes=[bk[1], ubS[s]])
                ctx.dma("sp", S32s[s][:, :, :], shg[s].rearrange("h k v -> k h v"), writes=[S32s[s]])
                hs = HState(S32s[s], Sbfs[s])
                op("act", I_act(hs.cur()[:, :, :], S32s[s][:, :, :], AF.Copy), reads=[S32s[s]], writes=[hs.cur()])
                hst.append(hs)
            segs = [(ubS[s], 94, s * 64, 64) for s in range(NS)]
            last = True
            out_conv = [ncs[s] for s in range(NS)]
            out_hg = [nhs[s] for s in range(NS)]
            out_y = ys.rearrange("s t d -> (s t) d")
        nxt = tiles[idx + 1] if idx + 1 < len(tiles) else None
        NGn = (4 if nxt[0] == "p" else 1) if nxt else 0
        xtn = xts[(idx + 1) % 2]
        if idx == 0:
            norm_stage(xt, NG, "ffn1_norm")
        ffn_up(1, NG, after_group=(lambda g: build_diag(g) if g < 4 else None) if idx == 0 else None)
        if nxt:
            load_x(idx + 1)
        pend = []

        def _flush(bank_list=None):
            while pend:
                norm_back(pend.pop(0), bank_list.pop(0) if bank_list else None)

        def _ap1(pp, tgs):
            for tg in tgs:
                norm_front(xt, tg, "mix_norm")
                pend.append(tg)

        ffn_down(1, xt, NG, after_mm=lambda pp: _flush(), after_pass=_ap1)

        def _at(tg):
            norm_front(xt, tg, "ffn2_norm")
            pend.append(tg)

        mixer(xt, NG, segs, hst, last, out_conv, out_hg, after_tg=_at, before_res=_flush, first_hook=_flush)
        ffn_up(2, NG, split=True, mid_hook=_flush)
        for tg in range(NGn):
            norm_front(xtn, tg, "ffn1_norm")

        def _after_mm(pp):
            if pp == 0:
                for tg in range(NGn):
                    norm_back(tg)

        ffn_down(2, xt, NG, after_mm=_after_mm, after_pass=lambda pp, tgs: [final_tg(xt, tg) for tg in tgs])
        ctx.dma("pool", out_y, xt[:, 0:4, :] if kind == "p" else xt[:, 0, :], reads=[xt.cells(0, NG * D)])

    ctx.emit()
    return nc, ctx


_CACHE = {}


def _get_program(SEQ):
    if SEQ not in _CACHE:
        _CACHE[SEQ] = build_program(SEQ)[0]
    return _CACHE[SEQ]


def kernel(**inputs):
    inp = {k: np.ascontiguousarray(np.asarray(v)) for k, v in inputs.items()}
    xp = inp["x_prompt"]
    B, SEQ, _ = xp.shape
    per = B // N_CORES
    nc = _get_program(SEQ)
    shared = {}
    shared["ffn1_w1"] = inp["ffn1_w1"][0]
    shared["ffn1_w3"] = inp["ffn1_w3"][0]
    shared["ffn1_w2"] = inp["ffn1_w2"][0]
    shared["ffn2_w1"] = inp["ffn2_w1"][0]
    shared["ffn2_w3"] = inp["ffn2_w3"][0]
    shared["ffn2_w2"] = inp["ffn2_w2"][0]
    shared["w_in"] = inp["w_in"][0]
    shared["w_out"] = inp["w_out"][0]
    shared["ffn1_norm"] = inp["ffn1_norm"][0]
    shared["mix_norm"] = inp["mix_norm"][0]
    shared["ffn2_norm"] = inp["ffn2_norm"][0]
    shared["final_norm"] = inp["final_norm"]
    shared["conv_dw_w"] = inp["conv_dw_w"][0]
    shared["conv_dw_b"] = inp["conv_dw_b"][0]
    shared["conv_ln_g"] = inp["conv_ln_g"][0]
    shared["conv_ln_b"] = inp["conv_ln_b"][0]
    shared["hg_lb_logits"] = inp["hg_lb_logits"]
    shared["hg_gnorm"] = inp["hg_gnorm"][0]
    in_maps = []
    for c in range(N_CORES):
        m = dict(shared)
        sl = slice(c * per, (c + 1) * per)
        m["xp"] = xp[sl]
        m["xs"] = inp["x_sample"][sl]
        m["sconv"] = inp["state_conv"][0, sl]
        m["shg"] = inp["state_hgrn"][0, sl]
        in_maps.append(m)
    res = run_bass_kernel_spmd(nc, in_maps, core_ids=list(range(N_CORES)))
    R = res.results
    y_prompt = np.concatenate([r["yp"] for r in R], axis=0)
    y_sample = np.concatenate([r["ys"] for r in R], axis=0)
    ncp = np.concatenate([r["ncp"] for r in R], axis=0)[None]
    nhp = np.concatenate([r["nhp"] for r in R], axis=0)[None]
    ncs = np.concatenate([r["ncs"] for r in R], axis=0)[None]
    nhs = np.concatenate([r["nhs"] for r in R], axis=0)[None]
    return (y_prompt.astype(np.float32), y_sample.astype(np.float32), ncp.astype(np.float32),
            nhp.astype(np.float32), ncs.astype(np.float32), nhs.astype(np.float32))
```

```python
import contextlib
import numpy as np
import concourse.bass as bass
import concourse.mybir as mybir
from concourse.bass_utils import run_bass_kernel_spmd

F32 = mybir.dt.float32
BF16 = mybir.dt.bfloat16
U8 = mybir.dt.uint8
AF = mybir.ActivationFunctionType
ALU = mybir.AluOpType

CELL = 256
DT_SIZE = {F32: 4, BF16: 2, U8: 1}

D = 1024
DFF = 2816
NJ = 22
DC = 512
EPS = 1e-6
N_CORES = 8


class Cell:
    __slots__ = ("last_w", "readers")

    def __init__(self):
        self.last_w = None
        self.readers = {}


class Tile:
    def __init__(self, ctx, name, shape, dtype, addr):
        self.ctx = ctx
        self.name = name
        self.shape = list(shape)
        self.dtype = dtype
        self.esz = DT_SIZE[dtype]
        self.free = int(np.prod(shape[1:]))
        self.nbytes = self.free * self.esz
        self.addr = addr
        v = ctx.arena[0:shape[0], addr:addr + self.nbytes].bitcast(dtype)
        if len(shape) == 3:
            v = v.rearrange("p (a b) -> p a b", b=shape[2])
        elif len(shape) == 4:
            v = v.rearrange("p (a b c) -> p a b c", b=shape[2], c=shape[3])
        self.ap = v

    def __getitem__(self, k):
        return self.ap[k]

    def cells(self, lo=None, n=None):
        if lo is None:
            lo, n = 0, self.free
        b0 = self.addr + lo * self.esz
        b1 = self.addr + (lo + n) * self.esz
        return self.ctx.cells[b0 // CELL:(b1 - 1) // CELL + 1]


class Rot:
    def __init__(self, tiles):
        self.tiles = tiles
        self.i = 0

    def next(self):
        t = self.tiles[self.i % len(self.tiles)]
        self.i += 1
        return t


class Ctx:
    ENGS = ["pe", "act", "dve", "pool", "sp"]

    def __init__(self, nc, arena_bytes, n_dma_sems):
        self.nc = nc
        self.arena_bytes = arena_bytes
        self.arena_t = nc.alloc_sbuf_tensor("arena", [128, arena_bytes], U8)
        self.arena = self.arena_t[:, :]
        self.cells = [Cell() for _ in range(arena_bytes // CELL + 1)]
        self.ptr = 0
        self.streams = {e: [] for e in self.ENGS}
        self.cnt = {e: 0 for e in self.ENGS}
        self.clock = {e: {} for e in self.ENGS}
        self.evclock = {}
        self.dma_pool = {q: ["d_%s_%d" % (q, i) for i in range(n)] for q, n in n_dma_sems.items()}
        self.dma_cnt = {}
        for q in self.dma_pool:
            for s in self.dma_pool[q]:
                self.dma_cnt[s] = 0
        self.dma_rr = {q: 0 for q in self.dma_pool}
        self.ninst = 0

    def alloc(self, name, shape, dtype, at=None):
        esz = DT_SIZE[dtype]
        nbytes = int(np.prod(shape[1:])) * esz
        if at is None:
            addr = (self.ptr + CELL - 1) // CELL * CELL
            self.ptr = addr + nbytes
            assert self.ptr <= self.arena_bytes, "SBUF arena overflow at %s: %d" % (name, self.ptr)
        else:
            addr = at
            assert addr + nbytes <= self.arena_bytes
        return Tile(self, name, shape, dtype, addr)

    def _merge_clock(self, eng, k, v):
        ck = self.clock[eng]
        ec = self.evclock.get((k, v))
        if ec:
            for kk, vv in ec.items():
                if ck.get(kk, 0) < vv:
                    ck[kk] = vv
        if ck.get(k, 0) < v:
            ck[k] = v

    def _deps(self, eng, reads, writes, is_dma):
        deps = {}

        def need(ev, kind):
            if ev is None:
                return
            k, v = ev
            if k == eng and not is_dma and eng == "pe":
                return
            if deps.get(k, 0) < v:
                deps[k] = v

        for c in reads:
            need(c.last_w, "raw")
        for c in writes:
            need(c.last_w, "waw")
            for k, v in c.readers.items():
                need((k, v), "war")
        ck = self.clock[eng]
        waits = [(k, v) for k, v in deps.items() if ck.get(k, 0) < v]
        for k, v in waits:
            self._merge_clock(eng, k, v)
        return waits

    def _mark(self, ev, reads, writes):
        k, v = ev
        for c in reads:
            if c.readers.get(k, 0) < v:
                c.readers[k] = v
        for c in writes:
            c.last_w = ev
            c.readers = {}

    @staticmethod
    def _flat(lst):
        out = []
        for x in lst:
            if isinstance(x, Cell):
                out.append(x)
            elif isinstance(x, Tile):
                out.extend(x.cells())
            else:
                out.extend(Ctx._flat(x))
        return out

    def op(self, eng, fns, reads=(), writes=()):
        if callable(fns):
            fns = [fns]
        reads = self._flat(reads)
        writes = self._flat(writes)
        waits = self._deps(eng, reads, writes, False)
        self.cnt[eng] += 1
        ev = (eng, self.cnt[eng])
        self.evclock[ev] = dict(self.clock[eng])
        self._mark(ev, reads, writes)
        self.streams[eng].append(("op", waits, fns, None))
        self.ninst += len(fns) + len(waits)
        return ev

    def dma(self, q, out_ap, in_ap, reads=(), writes=()):
        reads = self._flat(reads)
        writes = self._flat(writes)
        pool = self.dma_pool[q]
        sem = pool[self.dma_rr[q] % len(pool)]
        self.dma_rr[q] += 1
        waits = self._deps(q, reads, writes, True)
        prev = self.dma_cnt[sem]
        if prev > 0 and self.clock[q].get(sem, 0) < 16 * prev:
            waits.append((sem, 16 * prev))
            self._merge_clock(q, sem, 16 * prev)
        self.dma_cnt[sem] += 1
        ev = (sem, 16 * self.dma_cnt[sem])
        self.evclock[ev] = dict(self.clock[q])
        self._mark(ev, reads, writes)
        self.streams[q].append(("dma", waits, [lambda e, o=out_ap, i=in_ap: e.dma_start(out=o, in_=i)], sem))
        self.ninst += 1 + len(waits)
        return ev

    def emit(self):
        nc = self.nc
        with contextlib.ExitStack() as st:
            S = {}
            for e in self.ENGS:
                S[e] = st.enter_context(nc.semaphore("s_" + e))
            for q in self.dma_pool:
                for s in self.dma_pool[q]:
                    S[s] = st.enter_context(nc.semaphore(s))
            block = st.enter_context(nc.Block())
            finals = {e: [] for e in self.ENGS}
            for q in self.dma_pool:
                for s in self.dma_pool[q]:
                    if self.dma_cnt[s] > 0:
                        finals[q].append((s, 16 * self.dma_cnt[s]))

            def run(eng_name, eobj):
                for kind, waits, fns, sem in self.streams[eng_name]:
                    for k, v in waits:
                        eobj.wait_ge(S[k], v)
                    ins = None
                    for f in fns:
                        ins = f(eobj)
                    if kind == "op":
                        ins.then_inc(S[eng_name], 1)
                    else:
                        ins.then_inc(S[sem], 16)
                for k, v in finals[eng_name]:
                    eobj.wait_ge(S[k], v)

            @block.tensor
            def _(t):
                run("pe", t)

            @block.scalar
            def _(a):
                run("act", a)

            @block.vector
            def _(v):
                run("dve", v)

            @block.gpsimd
            def _(g):
                run("pool", g)

            @block.sync
            def _(s):
                run("sp", s)


def I_act(out, in_, func, **kw):
    return lambda e: e.activation(out=out, in_=in_, func=func, **kw)


def I_tt(out, in0, in1, op):
    return lambda e: e.tensor_tensor(out=out, in0=in0, in1=in1, op=op)


def I_ts(out, in0, s1, s2, op0, op1=None):
    if op1 is None:
        return lambda e: e.tensor_scalar(out=out, in0=in0, scalar1=s1, scalar2=None, op0=op0)
    return lambda e: e.tensor_scalar(out=out, in0=in0, scalar1=s1, scalar2=s2, op0=op0, op1=op1)


def I_stt(out, in0, scalar, in1, op0, op1):
    return lambda e: e.scalar_tensor_tensor(out=out, in0=in0, scalar=scalar, in1=in1, op0=op0, op1=op1)


def I_copy(out, in_):
    return lambda e: e.tensor_copy(out=out, in_=in_)


def I_memset(ap, val):
    return lambda e: e.memset(ap, val)


def I_mm(out, lhsT, rhs, start, stop, skip=False):
    if skip:
        return lambda e: e.matmul(out, lhsT=lhsT, rhs=rhs, start=start, stop=stop, skip_group_check=True)
    return lambda e: e.matmul(out, lhsT=lhsT, rhs=rhs, start=start, stop=stop)


def I_tr(out, in_, ident):
    return lambda e: e.transpose(out=out, in_=in_, identity=ident)


def I_scan(out, data):
    return lambda e: e.tensor_tensor_scan(out=out, data0=data, data1=data, initial=0.0,
                                          op0=ALU.add, op1=ALU.bypass)


def I_mscan(out, mask, data):
    return lambda e: e.tensor_tensor_scan(out=out, data0=mask, data1=data, initial=0.0,
                                          op0=ALU.mult, op1=ALU.add)


class HState:
    def __init__(self, S32, sbfs):
        self.S32 = S32
        self.sbfs = sbfs
        self.i = 0

    def cur(self):
        return self.sbfs[self.i % 2]

    def nxt(self):
        self.i += 1
        return self.sbfs[self.i % 2]


def build_program(SEQ, NP=2, NS=2, NR=5, do_sample=True, KS=31, KD=23):
    assert SEQ % 512 == 0
    nc = bass.Bass("TRN2", target_bir_lowering=False)

    def din(name, shape):
        return nc.dram_tensor(name, shape, F32, kind="ExternalInput").ap()

    def dout(name, shape):
        return nc.dram_tensor(name, shape, F32, kind="ExternalOutput").ap()

    xp = din("xp", [NP, SEQ, D])
    xs = din("xs", [NS, 64, D])
    sconv = din("sconv", [NS, 30, DC])
    shg = din("shg", [NS, 4, 128, 128])
    W = {}
    for f in (1, 2):
        W["w1", f] = din("ffn%d_w1" % f, [D, DFF])
        W["w3", f] = din("ffn%d_w3" % f, [D, DFF])
        W["w2", f] = din("ffn%d_w2" % f, [DFF, D])
    w_in = din("w_in", [D, 3072])
    w_out = din("w_out", [D, D])
    gains = {k: din(k, [D]) for k in ("ffn1_norm", "mix_norm", "ffn2_norm", "final_norm")}
    dw_w = din("conv_dw_w", [31, DC])
    dw_b = din("conv_dw_b", [DC])
    ln_g = din("conv_ln_g", [DC])
    ln_b = din("conv_ln_b", [DC])
    lbl = din("hg_lb_logits", [2, DC])
    gn = din("hg_gnorm", [DC])

    yp = dout("yp", [NP, SEQ, D])
    ys = dout("ys", [NS, 64, D])
    ncp = dout("ncp", [NP, 30, DC])
    nhp = dout("nhp", [NP, 4, 128, 128])
    ncs = dout("ncs", [NS, 30, DC])
    nhs = dout("nhs", [NS, 4, 128, 128])

    ARENA = 207 * 1024
    ctx = Ctx(nc, ARENA, {"sp": 24, "pool": 24, "act": 4})
    banks = []
    for i in range(8):
        t = nc.alloc_psum_tensor("bank%d" % i, [128, 512], F32)
        banks.append((t, Cell()))
    bank_i = [0]

    def nb():
        b = banks[bank_i[0] % 8]
        bank_i[0] += 1
        return b

    slabs = {}

    def new_slab(key, a, b):
        slabs[key] = (len(slabs), a, b, Cell())

    for f in (1, 2):
        for g in range(6):
            nch = 4 if g < 5 else 2
            new_slab(("w1", f, g), 8, nch * 128)
            new_slab(("w3", f, g), 8, nch * 128)
        for g in range(6):
            njc = 4 if g < 5 else 2
            new_slab(("w2", f, g), njc, 1024)
    for s in range(6):
        new_slab(("win", s), 8, 512)
    for h in range(2):
        new_slab(("wout", h), 8, 512)
    for m in range(4):
        new_slab(("cv", m), 31, 128)
    NSLAB = len(slabs)
    wsc = nc.dram_tensor("wsc", [NSLAB, 128, 4096], BF16, kind="Internal").ap()

    def slab_dram(key):
        idx, a, b, _ = slabs[key]
        return wsc[idx][:, 0:a * b].rearrange("p (a b) -> p a b", b=b)

    A0, G0, Q0, F0, I0, GG0 = 0, 512, 1024, 1536, 2048, 2560
    win_cols = {
        0: [A0, A0 + 128, G0, G0 + 128],
        1: [A0 + 256, A0 + 384, G0 + 256, G0 + 384],
        2: [Q0, Q0 + 128, F0, F0 + 128],
        3: [Q0 + 256, Q0 + 384, F0 + 256, F0 + 384],
        4: [I0, I0 + 128, I0 + 256, I0 + 384],
        5: [GG0, GG0 + 128, GG0 + 256, GG0 + 384],
    }

    def fp32_pieces(key):
        idx, a, b, cell = slabs[key]
        kind = key[0]
        if kind in ("w1", "w3"):
            _, f, g = key
            return [(0, b, W[kind, f].rearrange("(kc p) n -> p kc n", p=128)[:, :, g * 512:g * 512 + b])]
        if kind == "w2":
            _, f, g = key
            return [(0, b, W["w2", f][g * 512:g * 512 + a * 128, :].rearrange("(jc p) n -> p jc n", p=128))]
        if kind == "wout":
            h = key[1]
            return [(0, b, w_out.rearrange("(kc p) n -> p kc n", p=128)[:, :, h * 512:(h + 1) * 512])]
        assert kind == "win"
        cols = win_cols[key[1]]
        srcv = w_in.rearrange("(kc p) n -> p kc n", p=128)
        out = []
        i = 0
        while i < 4:
            j = i
            while j + 1 < 4 and cols[j + 1] == cols[j] + 128:
                j += 1
            n = (j - i + 1) * 128
            out.append((i * 128, n, srcv[:, :, cols[i]:cols[i] + n]))
            i = j + 1
        return out

    converted = set()

    al = ctx.alloc
    ones = al("ones", [128, 128], F32)
    ident = al("ident", [128, 128], F32)
    identb = al("identb", [128, 128], BF16)
    mask4 = al("mask4", [128, 4, 128], F32)
    ones_dv = al("ones_dv", [128, 128], F32)
    ones_c = al("ones_c", [128, 128], F32)
    epsc = al("epsc", [128, 8], F32)
    cw = al("cw", [128, 124], F32)
    pv = al("pv", [128, 24], F32)
    lbp = al("lbp", [128, 16], F32)
    stg1 = al("stg1", [128, 128], F32)
    stg2 = al("stg2", [128, 128], F32)
    ss = al("ss", [128, 8], F32)
    ms = al("ms", [128, 8], F32)
    rstd = al("rstd", [128, 8], F32)
    ebl = al("ebl", [128, 4, 8, 1], F32)
    gbc = {k: al("gbc_" + k, [128, D], F32) for k in gains}
    xts = [al("xt%d" % i, [128, 4, D], F32) for i in range(2)]
    ubuf = al("ubuf", [128, 4, 542], BF16)
    u32s = [al("u32l%d" % i, [128, 4, 32], F32) for i in range(2)]
    ubS = [al("ubS%d" % i, [128, 4, 94], BF16) for i in range(NS)]
    S32s = [al("S32_%d" % i, [128, 4, 128], F32) for i in range(2)]
    Sbfs = [[al("Sbf_%d_%d" % (i, j), [128, 4, 128], BF16) for j in range(2)] for i in range(2)]
    v_tm = al("v_tm", [128, 4, 512], BF16)
    gsn = al("gsn", [128, 4, 512], F32)
    r_fm = al("r_fm", [128, 4, 512], BF16)
    ring = [al("ring%d" % i, [128, 4096], BF16) for i in range(NR)]
    HT = ctx.ptr = (ctx.ptr + CELL - 1) // CELL * CELL
    h_tm = al("h_tm", [128, 4, D], BF16)
    qt = al("qt", [128, 4, 512], BF16, at=HT)
    kt = al("kt", [128, 4, 512], BF16, at=HT + 4096)
    HF = ctx.ptr = (ctx.ptr + CELL - 1) // CELL * CELL
    h_fm = al("h_fm", [128, 8, 512], BF16)
    kh = al("kh", [128, 4, 512], BF16, at=HF)
    c_fm = al("c_fm", [128, 4, 512], BF16, at=HF + 4096)
    MT = ctx.ptr = (ctx.ptr + CELL - 1) // CELL * CELL
    g_fm = al("g_fm", [128, NJ, 512], BF16)
    sa = Rot([al("sa%d" % i, [128, 512], F32) for i in range(2)])
    p = MT
    ah_t = []
    for i in range(3):
        ah_t.append(al("ah%d" % i, [128, 512], F32, at=p)); p += 2048
    th_t = []
    for i in range(2):
        th_t.append(al("th%d" % i, [128, 512], F32, at=p)); p += 2048
    qs = al("qs", [128, 4, 512], F32, at=p); p += 8192
    thf = al("thf", [128, 4, 512], F32, at=p); p += 8192
    kk = al("kk", [128, 4, 512], F32, at=p); p += 8192
    b_t = []
    for i in range(2):
        b_t.append(al("bb%d" % i, [128, 512], F32, at=p)); p += 2048
    E_t = []
    for i in range(3):
        E_t.append(al("E%d" % i, [128, 512], F32, at=p)); p += 2048
    D_t = []
    for i in range(2):
        D_t.append(al("Dd%d" % i, [128, 512], F32, at=p)); p += 2048
    MT_END1 = p
    p = MT
    ycv = al("ycv", [128, 4, 512], F32, at=p); p += 8192
    p += 4096
    p += 4096
    tt_t = []
    for i in range(2):
        tt_t.append(al("tt%d" % i, [128, 512], F32, at=p)); p += 2048
    Am_t = []
    for i in range(2):
        Am_t.append(al("Am%d" % i, [128, 4, 128], BF16, at=p)); p += 1024
    khT_t = []
    for i in range(2):
        khT_t.append(al("khT%d" % i, [128, 4, 128], BF16, at=p)); p += 1024
    osq_t = []
    for i in range(2):
        osq_t.append(al("osq%d" % i, [128, 512], BF16, at=p)); p += 2048
    osb_t = []
    for i in range(2):
        osb_t.append(al("osb%d" % i, [128, 512], F32, at=p)); p += 2048
    tv2 = al("tv2", [128, 512], F32, at=p); p += 2048
    rs2 = al("rs2", [128, 512], F32, at=p); p += 2048
    t1 = al("t1", [128, 512], F32, at=p); p += 2048
    MT_END2 = p
    ctx.ptr = max(ctx.ptr, MT_END1, MT_END2)
    dgs = al("dgs", [128, 31, 128], BF16)
    fjunk = al("fjunk", [128, D], BF16, at=dgs.addr)
    tv = al("tv", [128, 512], F32, at=dgs.addr + 2048)
    stgc = al("stgc", [128, 512], F32, at=dgs.addr + 2048)
    qp = al("qp", [128, 4, 256], BF16)
    ones_dvb = al("ones_dvb", [128, 128], BF16)
    ones_cb = al("ones_cb", [128, 128], BF16)
    cmask = al("cmask", [128, 512], BF16)
    rs = al("rs", [128, 512], F32, at=dgs.addr + 4096)
    sq_t = [al("sqA", [128, 512], BF16, at=MT + 8192), al("sqB", [128, 512], BF16)]
    b_t.append(al("bb2", [128, 512], F32))
    D_t.append(al("Dd2", [128, 512], F32))
    assert ctx.ptr <= ARENA, ctx.ptr
    tt4 = [tt_t[0], tt_t[1], osq_t[0], osq_t[1]]
    ah_r, th_r, b_r, E_r, D_r = Rot(ah_t), Rot(th_t), Rot(b_t), Rot(E_t), Rot(D_t)
    sq_r, tt_r, Am_r, khT_r, osq_r, osb_r = Rot(sq_t), Rot(tt_t), Rot(Am_t), Rot(khT_t), Rot(osq_t), Rot(osb_t)

    op = ctx.op
    op("pool", I_memset(ones[:, :], 1.0), writes=[ones])
    op("pool", lambda e: e.affine_select(out=ident[:, :], in_=ones[:, :], pattern=[[-1, 128]],
                                         compare_op=ALU.is_equal, fill=0.0, base=0, channel_multiplier=1),
       reads=[ones], writes=[ident])
    for hh in range(4):
        op("pool", lambda e, hh=hh: e.affine_select(out=mask4[:, hh, :], in_=ones[:, :], pattern=[[1, 128]],
                                                     compare_op=ALU.is_ge, fill=0.0, base=0, channel_multiplier=-1),
           reads=[ones], writes=[mask4])
    op("pool", I_memset(ones_dv[:, :], 1.0 / 128), writes=[ones_dv])
    op("pool", I_memset(ones_c[:, :], 1.0 / 512), writes=[ones_c])
    op("pool", I_memset(ones_dvb[:, :], 1.0 / 128), writes=[ones_dvb])
    op("pool", I_memset(cmask[:, :], 1.0), writes=[cmask])
    op("pool", I_memset(cmask[:, :].rearrange("p (c j) -> p c j", j=64)[:, :, 0:1], 0.0), writes=[cmask])
    op("pool", I_memset(ones_cb[:, :], 1.0 / 512), writes=[ones_cb])
    op("pool", I_memset(epsc[:, :], EPS), writes=[epsc])
    op("dve", I_copy(identb[:, :], ident[:, :]), reads=[ident], writes=[identb])
    ctx.dma("sp", stg1[0:124, :], dw_w.rearrange("k (c p) -> (k c) p", p=128), writes=[stg1])
    for r0, vec in ((0, dw_b), (4, ln_g), (8, ln_b), (12, gn)):
        ctx.dma("sp", stg2[r0:r0 + 4, :], vec.rearrange("(c p) -> c p", p=128), writes=[stg2])
    ctx.dma("sp", stg2[16:24, :], lbl.rearrange("r (c p) -> (r c) p", p=128), writes=[stg2])
    for k, g_ap in gains.items():
        ctx.dma("sp", gbc[k][:, :], g_ap.partition_broadcast(128), writes=[gbc[k]])
    b = nb()
    op("pe", I_tr(b[0][:, 0:124], stg1[0:124, :], ident[0:124, 0:124]), reads=[stg1, ident], writes=[b[1]])
    op("dve", I_copy(cw[:, :], b[0][:, 0:124]), writes=[b[1], cw])
    b = nb()
    op("pe", I_tr(b[0][:, 0:24], stg2[0:24, :], ident[0:24, 0:24]), reads=[stg2, ident], writes=[b[1]])
    op("dve", I_copy(pv[:, :], b[0][:, 0:24]), writes=[b[1], pv])
    op("dve", I_tt(lbp[:, 0:4], pv[:, 16:20], pv[:, 20:24], ALU.subtract), reads=[pv], writes=[lbp])
    op("act", I_act(lbp[:, 0:4], lbp[:, 0:4], AF.Tanh, scale=0.5), reads=[lbp], writes=[lbp])
    op("dve", I_ts(lbp[:, 4:8], lbp[:, 0:4], -0.25, 0.25, ALU.mult, ALU.add), reads=[lbp], writes=[lbp])
    op("dve", I_ts(lbp[:, 8:12], lbp[:, 0:4], 0.25, 0.75, ALU.mult, ALU.add), reads=[lbp], writes=[lbp])
    op("dve", I_ts(lbp[:, 12:16], lbp[:, 0:4], 0.25, -0.25, ALU.mult, ALU.add), reads=[lbp], writes=[lbp])

    def C1(m):
        return lbp[:, 4 + m:5 + m]

    def C0(m):
        return lbp[:, 8 + m:9 + m]

    def NC1(m):
        return lbp[:, 12 + m:13 + m]

    def build_diag(m):
        for k in range(31):
            op("act", I_act(dgs[:, k, :], identb[:, :], AF.Identity, scale=cw[:, k * 4 + m:k * 4 + m + 1]),
               reads=[identb, cw], writes=[dgs.cells(k * 128, 128)])
        ctx.dma("act", slab_dram(("cv", m)), dgs[:, :, :], reads=[dgs], writes=[slabs[("cv", m)][3]])

    ring_i = [0]

    def load_slab(key):
        idx, a, b_, cell = slabs[key]
        t = ring[ring_i[0] % NR]
        ring_i[0] += 1
        dst = t.ap[:, 0:a * b_].rearrange("p (a b) -> p a b", b=b_)
        cells = t.cells(0, a * b_)
        if key[0] != "cv" and key not in converted:
            converted.add(key)
            for (c_lo, n, src) in fp32_pieces(key):
                ctx.dma("pool", dst[:, :, c_lo:c_lo + n], src, writes=[cells])
            ctx.dma("sp", slab_dram(key), dst, reads=[cells], writes=[cell])
        else:
            ctx.dma("sp", dst, slab_dram(key), reads=[cell], writes=[cells])
        return cells, dst

    def norm_front(xt, tg, gk):
        op("dve", I_memset(ss[:, tg:tg + 1], 0.0), writes=[ss])
        op("act", I_act(h_tm[:, tg, :], xt[:, tg, :], AF.Square, accum_out=ss[:, tg:tg + 1]),
           reads=[xt.cells(tg * D, D)], writes=[h_tm.cells(tg * D, D), ss])
        op("act", I_act(ms[:, tg:tg + 1], ss[:, tg:tg + 1], AF.Ln, scale=1.0 / D, bias=epsc[:, 0:1]),
           reads=[ss, epsc], writes=[ms])
        op("act", I_act(rstd[:, tg:tg + 1], ms[:, tg:tg + 1], AF.Exp, scale=-0.5), reads=[ms], writes=[rstd])
        op("dve", I_stt(h_tm[:, tg, :], xt[:, tg, :], rstd[:, tg:tg + 1], gbc[gk][:, :], ALU.mult, ALU.mult),
           reads=[xt.cells(tg * D, D), rstd, gbc[gk]], writes=[h_tm.cells(tg * D, D)])

    def norm_back(tg, bank=None):
        bk = bank if bank is not None else nb()
        bv = bk[0][:, :].bitcast(BF16)
        op("pe", [I_tr(bv[:, fc * 128:(fc + 1) * 128], h_tm[:, tg, fc * 128:(fc + 1) * 128], identb[:, :])
                  for fc in range(8)], reads=[h_tm.cells(tg * D, D), identb], writes=[bk[1]])
        src = bv[:, :].rearrange("p (f t) -> p f t", t=128)
        dst = h_fm[:, :, tg * 128:(tg + 1) * 128]
        hc = [h_fm.cells(fc * 512 + tg * 128, 128) for fc in range(8)]
        if tg % 2 == 0:
            op("act", I_act(dst, src, AF.Copy), writes=[bk[1], hc])
        else:
            op("dve", I_copy(dst, src), writes=[bk[1], hc])

    def norm_stage(xt, NG, gk):
        for tg in range(NG):
            norm_front(xt, tg, gk)
        for tg in range(NG):
            norm_back(tg)

    def ffn_up(f, NG, after_group=None, split=False, mid_hook=None):
        ntok = NG * 128
        for g in range(6):
            nch = 4 if g < 5 else 2
            c1_, v1 = load_slab(("w1", f, g))
            c3_, v3 = load_slab(("w3", f, g))
            if g == 0 and not (split and NG == 4) and mid_hook is not None:
                mid_hook()
            if split and g == 0 and NG == 4:
                nsp = 3
                bks = [(nb(), nb()) for i in range(nsp)]
                for (lo, n) in ((0, 384), (384, 128)):
                    if lo > 0 and mid_hook is not None:
                        mid_hook()
                    hc = [h_fm.cells(kc * 512 + lo, n) for kc in range(8)]
                    for i in range(nsp):
                        bA, bB = bks[i]
                        op("pe", [I_mm(bA[0][:, lo:lo + n], v1[:, kc, i * 128:(i + 1) * 128], h_fm[:, kc, lo:lo + n],
                                       kc == 0, kc == 7) for kc in range(8)], reads=[c1_, hc], writes=[bA[1]])
                        op("pe", [I_mm(bB[0][:, lo:lo + n], v3[:, kc, i * 128:(i + 1) * 128], h_fm[:, kc, lo:lo + n],
                                       kc == 0, kc == 7) for kc in range(8)], reads=[c3_, hc], writes=[bB[1]])
                for i in range(nch):
                    j = g * 4 + i
                    if i < nsp:
                        bA, bB = bks[i]
                    else:
                        bA = nb()
                        bB = nb()
                        op("pe", [I_mm(bA[0][:, 0:ntok], v1[:, kc, i * 128:(i + 1) * 128], h_fm[:, kc, 0:ntok], kc == 0, kc == 7)
                                  for kc in range(8)], reads=[c1_, h_fm], writes=[bA[1]])
                        op("pe", [I_mm(bB[0][:, 0:ntok], v3[:, kc, i * 128:(i + 1) * 128], h_fm[:, kc, 0:ntok], kc == 0, kc == 7)
                                  for kc in range(8)], reads=[c3_, h_fm], writes=[bB[1]])
                    s = sa.next()
                    op("act", I_act(s[:, 0:ntok], bA[0][:, 0:ntok], AF.Silu), writes=[bA[1], s])
                    op("dve", I_tt(g_fm[:, j, 0:ntok], s[:, 0:ntok], bB[0][:, 0:ntok], ALU.mult),
                       reads=[s], writes=[bB[1], g_fm.cells(j * 512, ntok)])
                if after_group is not None:
                    after_group(g)
                continue
            for i in range(nch):
                j = g * 4 + i
                bA = nb()
                bB = nb()
                op("pe", [I_mm(bA[0][:, 0:ntok], v1[:, kc, i * 128:(i + 1) * 128], h_fm[:, kc, 0:ntok], kc == 0, kc == 7)
                          for kc in range(8)], reads=[c1_, h_fm], writes=[bA[1]])
                op("pe", [I_mm(bB[0][:, 0:ntok], v3[:, kc, i * 128:(i + 1) * 128], h_fm[:, kc, 0:ntok], kc == 0, kc == 7)
                          for kc in range(8)], reads=[c3_, h_fm], writes=[bB[1]])
                s = sa.next()
                op("act", I_act(s[:, 0:ntok], bA[0][:, 0:ntok], AF.Silu), writes=[bA[1], s])
                op("dve", I_tt(g_fm[:, j, 0:ntok], s[:, 0:ntok], bB[0][:, 0:ntok], ALU.mult),
                   reads=[s], writes=[bB[1], g_fm.cells(j * 512, ntok)])
            if after_group is not None:
                after_group(g)

    def ffn_down(f, xt, NG, after_mm=None, after_pass=None):
        if NG == 4:
            passes = [[0, 1, 2], [3]] if f == 1 else [[0, 1], [2, 3]]
        else:
            passes = [list(range(NG))]
        for pp, tgs in enumerate(passes):
            accs = {}
            for tg in tgs:
                for ch in range(2):
                    accs[(tg, ch)] = nb()
            for g in range(6):
                njc = 4 if g < 5 else 2
                c2_, v2 = load_slab(("w2", f, g))
                fns = []
                for jc in range(njc):
                    j = g * 4 + jc
                    for tg in tgs:
                        for ch in range(2):
                            fns.append(I_mm(accs[(tg, ch)][0][:, :], g_fm[:, j, tg * 128:(tg + 1) * 128],
                                            v2[:, jc, ch * 512:(ch + 1) * 512], j == 0, j == NJ - 1))
                op("pe", fns, reads=[c2_, g_fm], writes=[a[1] for a in accs.values()])
            for (tg, ch), a in accs.items():
                xv = xt[:, tg, ch * 512:(ch + 1) * 512]
                op("dve", I_stt(xv, a[0][:, :], 0.5, xv, ALU.mult, ALU.add),
                   reads=[xt.cells(tg * D + ch * 512, 512)], writes=[a[1], xt.cells(tg * D + ch * 512, 512)])
            if after_mm is not None:
                after_mm(pp)
            if after_pass is not None:
                after_pass(pp, tgs)

    def mixer(xt, NG, segs, hst, last, out_conv, out_hg, after_tg=None, before_res=None, first_hook=None):
        ntok = NG * 128
        nchunk = ntok // 64
        for sl in range(2):
            cs, v = load_slab(("win", sl))
            ahs = []
            pre_bks = None
            if sl == 0 and NG != 4 and first_hook is not None:
                first_hook()
            if sl == 0 and NG == 4:
                pre_bks = [nb() for i in range(4)]
                for (lo, n) in ((0, 384), (384, 128)):
                    if lo > 0 and first_hook is not None:
                        first_hook()
                    hc = [h_fm.cells(kc * 512 + lo, n) for kc in range(8)]
                    for i in range(4):
                        bk = pre_bks[i]
                        op("pe", [I_mm(bk[0][:, lo:lo + n], v[:, kc, i * 128:(i + 1) * 128], h_fm[:, kc, lo:lo + n],
                                       kc == 0, kc == 7) for kc in range(8)], reads=[cs, hc], writes=[bk[1]])
            for i in range(4):
                if pre_bks is not None:
                    bk = pre_bks[i]
                else:
                    bk = nb()
                    op("pe", [I_mm(bk[0][:, 0:ntok], v[:, kc, i * 128:(i + 1) * 128], h_fm[:, kc, 0:ntok], kc == 0, kc == 7)
                              for kc in range(8)], reads=[cs, h_fm], writes=[bk[1]])
                if i < 2:
                    a = ah_r.next()
                    ahs.append(a)
                    op("act", I_act(a[:, 0:ntok], bk[0][:, 0:ntok], AF.Identity, scale=0.5), writes=[bk[1], a])
                else:
                    m = 2 * sl + (i - 2)
                    h = th_r.next()
                    op("act", I_act(h[:, 0:ntok], bk[0][:, 0:ntok], AF.Tanh, scale=0.5), writes=[bk[1], h])
                    for si, (ub, Wd, c0, L) in enumerate(segs):
                        op("dve", I_stt(ub[:, m, 30:30 + L], h[:, c0:c0 + L], 1.0, ahs[i - 2][:, c0:c0 + L], ALU.add, ALU.mult),
                           reads=[h, ahs[i - 2]], writes=[ub.cells(m * Wd + 30, L)])
                        if last:
                            e0 = c0 + L - 30
                            op("dve", I_stt(u32s[si][:, m, 0:30], h[:, e0:e0 + 30], 1.0, ahs[i - 2][:, e0:e0 + 30], ALU.add, ALU.mult),
                               reads=[h, ahs[i - 2]], writes=[u32s[si]])
        dve_bg = []
        for (ub, Wd, c0, L) in segs:
            for k in range(KD, 31):
                for m in range(4):
                    yv = ycv[:, m, c0:c0 + L]
                    if k == KD:
                        dve_bg.append((I_ts(yv, ub[:, m, k:k + L], cw[:, k * 4 + m:k * 4 + m + 1], None, ALU.mult),
                                       [ub.cells(m * Wd, Wd), cw], [ycv.cells(m * 512 + c0, L)]))
                    else:
                        dve_bg.append((I_stt(yv, ub[:, m, k:k + L], cw[:, k * 4 + m:k * 4 + m + 1], yv, ALU.mult, ALU.add),
                                       [ub.cells(m * Wd, Wd), ycv.cells(m * 512 + c0, L), cw], [ycv.cells(m * 512 + c0, L)]))

        def bgd(n):
            for _ in range(n):
                if dve_bg:
                    f_, r_, w_ = dve_bg.pop(0)
                    op("dve", f_, reads=r_, writes=w_)

        for sl in (2, 3):
            cs, v = load_slab(("win", sl))
            for i in range(4):
                m = 2 * (sl - 2) + i % 2
                bk = nb()
                op("pe", [I_mm(bk[0][:, 0:ntok], v[:, kc, i * 128:(i + 1) * 128], h_fm[:, kc, 0:ntok], kc == 0, kc == 7)
                          for kc in range(8)], reads=[cs, h_fm], writes=[bk[1]])
                if i < 2:
                    op("act", I_act(qs[:, m, 0:ntok], bk[0][:, 0:ntok], AF.Silu), writes=[bk[1], qs.cells(m * 512, ntok)])
                else:
                    op("act", I_act(thf[:, m, 0:ntok], bk[0][:, 0:ntok], AF.Tanh, scale=0.5),
                       writes=[bk[1], thf.cells(m * 512, ntok)])
                    op("dve", I_ts(kk[:, m, 0:ntok], thf[:, m, 0:ntok], NC1(m), C1(m), ALU.mult, ALU.add),
                       reads=[thf.cells(m * 512, ntok), lbp], writes=[kk.cells(m * 512, ntok)])
                bgd(4)
        cs, v = load_slab(("win", 5))
        for m in range(4):
            bk = nb()
            op("pe", [I_mm(bk[0][:, 0:ntok], v[:, kc, m * 128:(m + 1) * 128], h_fm[:, kc, 0:ntok], kc == 0, kc == 7)
                      for kc in range(8)], reads=[cs, h_fm], writes=[bk[1]])
            op("act", I_act(gsn[:, m, 0:ntok], bk[0][:, 0:ntok], AF.Silu), writes=[bk[1], gsn.cells(m * 512, ntok)])
            op("dve", I_ts(gsn[:, m, 0:ntok], gsn[:, m, 0:ntok], pv[:, 12 + m:13 + m], None, ALU.mult),
               reads=[gsn.cells(m * 512, ntok), pv], writes=[gsn.cells(m * 512, ntok)])
            bgd(2)
        cs, v = load_slab(("win", 4))
        for tg in range(NG):
            bk = nb()
            op("pe", [I_mm(bk[0][:, :], h_fm[:, kc, tg * 128:(tg + 1) * 128], v[:, kc, :], kc == 0, kc == 7)
                      for kc in range(8)], reads=[cs, h_fm], writes=[bk[1]])
            op("act", I_act(v_tm[:, tg, :], bk[0][:, :], AF.Copy), writes=[bk[1], v_tm.cells(tg * 512, 512)])
            bgd(2)
        bgd(10 ** 6)
        for m in range(4):
            fm = thf[:, m, 0:ntok]
            fcell = thf.cells(m * 512, ntok)
            op("act", I_act(fm, fm, AF.Ln, scale=C1(m), bias=C0(m)), reads=[fcell, lbp], writes=[fcell])
        bbs, dds = {}, {}

        def prep_scan(m):
            fcell = thf.cells(m * 512, ntok)
            bb = b_r.next()
            bbs[m] = bb
            op("dve", I_mscan(bb[:, 0:ntok], cmask[:, 0:ntok], thf[:, m, 0:ntok]),
               reads=[fcell, cmask], writes=[bb.cells(0, ntok)])
            bb3 = bb[:, 0:ntok].rearrange("p (c j) -> p c j", j=64)
            dd = D_r.next()
            dds[m] = dd
            dd3 = dd[:, 0:ntok].rearrange("p (c j) -> p c j", j=64)
            op("dve", I_tt(dd3, bb3[:, :, 63:64].to_broadcast([128, nchunk, 64]), bb3, ALU.subtract),
               reads=[bb], writes=[dd])

        def prep_exp(m):
            bb, dd = bbs[m], dds[m]
            bb3 = bb[:, 0:ntok].rearrange("p (c j) -> p c j", j=64)
            e1 = E_r.next()
            op("act", I_act(e1[:, 0:ntok], bb[:, 0:ntok], AF.Exp), reads=[bb], writes=[e1])
            e2 = E_r.next()
            op("act", I_act(e2[:, 0:ntok], bb[:, 0:ntok], AF.Exp, scale=-1.0), reads=[bb], writes=[e2])
            e3 = E_r.next()
            op("act", I_act(e3[:, 0:ntok], dd[:, 0:ntok], AF.Exp), reads=[dd], writes=[e3])
            op("act", I_act(ebl[:, m, 0:nchunk, :], bb3[:, :, 63:64], AF.Exp), reads=[bb], writes=[ebl])
            op("dve", I_tt(qt[:, m, 0:ntok], qs[:, m, 0:ntok], e1[:, 0:ntok], ALU.mult),
               reads=[qs.cells(m * 512, ntok), e1], writes=[qt.cells(m * 512, ntok)])
            op("dve", I_tt(kt[:, m, 0:ntok], kk[:, m, 0:ntok], e2[:, 0:ntok], ALU.mult),
               reads=[kk.cells(m * 512, ntok), e2], writes=[kt.cells(m * 512, ntok)])
            op("dve", I_tt(kh[:, m, 0:ntok], kk[:, m, 0:ntok], e3[:, 0:ntok], ALU.mult),
               reads=[kk.cells(m * 512, ntok), e3], writes=[kh.cells(m * 512, ntok)])
            npair = ntok // 128
            q4 = qt[:, m, 0:ntok].rearrange("p (a two j) -> p a two j", two=2, j=64)
            e4 = ebl[:, m, 0:nchunk, :].rearrange("p (a two) o -> p a two o", two=2)
            op("dve", I_tt(qp[:, m, 0:npair * 64].rearrange("p (a j) -> p a j", j=64), q4[:, :, 1, :],
                           e4[:, :, 0, :].to_broadcast([128, npair, 64]), ALU.mult),
               reads=[qt.cells(m * 512, ntok), ebl], writes=[qp.cells(m * 256, npair * 64)])

        prep_scan(0)
        prep_scan(1)
        prep_exp(0)
        prep_scan(2)
        prep_exp(1)
        prep_scan(3)
        prep_exp(2)
        prep_exp(3)
        for m in range(4):
            cs, v = load_slab(("cv", m))
            bk = nb()
            fns = []
            first = True
            for (ub, Wd, c0, L) in segs:
                for k in range(KD):
                    fns.append(I_mm(bk[0][:, c0:c0 + L], v[:, k, :], ub[:, m, k:k + L], first,
                                    (KD == 31 and k == KD - 1 and c0 + L == ntok), skip=True))
                    first = False
            rds = [cs] + [ub.cells(m * Wd, Wd) for (ub, Wd, c0, L) in segs]
            if KD < 31:
                fns.append(I_mm(bk[0][:, 0:ntok], ident[:, :], ycv[:, m, 0:ntok], False, True, skip=True))
                rds += [ident, ycv.cells(m * 512, ntok)]
            op("pe", fns, reads=rds, writes=[bk[1]])
            op("act", I_act(ycv[:, m, 0:ntok], bk[0][:, 0:ntok], AF.Identity, bias=pv[:, m:m + 1]),
               reads=[pv], writes=[bk[1], ycv.cells(m * 512, ntok)])
        if last:
            pass

        def bg(n):
            return

        for si, (ub, Wd, c0, L) in enumerate(segs):
            if not last:
                op("pool", I_copy(ub[:, :, 0:30], ub[:, :, L:L + 30]), reads=[ub], writes=[ub])
            else:
                bk = nb()
                op("pe", [I_tr(bk[0][0:30, m * 128:(m + 1) * 128], u32s[si][:, m, 0:30], ident[:, :]) for m in range(4)],
                   reads=[u32s[si], ident], writes=[bk[1]])
                op("act", I_act(stgc[0:30, :], bk[0][0:30, :], AF.Copy), writes=[bk[1], stgc])
                ctx.dma("pool", out_conv[si], stgc[0:30, :], reads=[stgc])
        def ln_stage(k):
            if k == 0:
                bm = banks[6]
                for m in range(4):
                    op("pe", I_mm(bm[0][:, 0:ntok], ones_c[:, :], ycv[:, m, 0:ntok], m == 0, m == 3),
                       reads=[ones_c, ycv.cells(m * 512, ntok)], writes=[bm[1]])
                for m in range(4):
                    yv = ycv[:, m, 0:ntok]
                    op("dve", I_tt(yv, yv, bm[0][:, 0:ntok], ALU.subtract),
                       reads=[ycv.cells(m * 512, ntok)], writes=[bm[1], ycv.cells(m * 512, ntok)])
            elif k == 1:
                bvv = banks[7]
                for m in range(4):
                    sq = sq_r.next()
                    op("act", I_act(sq[:, 0:ntok], ycv[:, m, 0:ntok], AF.Square), reads=[ycv.cells(m * 512, ntok)], writes=[sq])
                    op("pe", I_mm(bvv[0][:, 0:ntok], ones_cb[:, :], sq[:, 0:ntok], m == 0, m == 3),
                       reads=[ones_cb, sq], writes=[bvv[1]])
                op("act", I_act(tv[:, 0:ntok], bvv[0][:, 0:ntok], AF.Ln, bias=epsc[:, 0:1]), reads=[epsc], writes=[bvv[1], tv])
                op("act", I_act(rs[:, 0:ntok], tv[:, 0:ntok], AF.Exp, scale=-0.5), reads=[tv], writes=[rs])
            elif k == 2:
                for m in range(4):
                    yv = ycv[:, m, 0:ntok]
                    op("dve", I_tt(yv, yv, rs[:, 0:ntok], ALU.mult),
                       reads=[ycv.cells(m * 512, ntok), rs], writes=[ycv.cells(m * 512, ntok)])
            else:
                for m in range(4):
                    op("act", I_act(c_fm[:, m, 0:ntok], ycv[:, m, 0:ntok], AF.Silu, scale=pv[:, 4 + m:5 + m],
                                    bias=pv[:, 8 + m:9 + m]),
                       reads=[ycv.cells(m * 512, ntok), pv], writes=[c_fm.cells(m * 512, ntok)])

        def hg_front(pr):
            c0 = pr * 128
            paired = hst[2 * pr] is hst[2 * pr + 1]
            bA = banks[0]
            fns = []
            for hh in range(4):
                fns.append(I_mm(bA[0][:, hh * 128:(hh + 1) * 128], kt[:, hh, c0:c0 + 128], qt[:, hh, c0:c0 + 128], True, True))
                if paired:
                    fns.append(I_mm(bA[0][0:64, hh * 128 + 64:(hh + 1) * 128], kh[:, hh, c0:c0 + 64],
                                    qt[:, hh, c0 + 64:c0 + 128], True, True))
            op("pe", fns, reads=[kt, qt, kh], writes=[bA[1]])
            am = Am_r.next()
            op("dve", I_tt(am[:, :, :], bA[0][:, :].rearrange("p (h t) -> p h t", t=128), mask4[:, :, :], ALU.mult),
               reads=[mask4], writes=[bA[1], am])
            if not paired:
                op("dve", I_memset(am[0:64, :, 64:128], 0.0), writes=[am])
            bT = banks[1]
            bTv = bT[0][:, :].bitcast(BF16)
            op("pe", [I_tr(bTv[:, hh * 128:(hh + 1) * 128], kh[:, hh, c0:c0 + 128], identb[:, :]) for hh in range(4)],
               reads=[kh, identb], writes=[bT[1]])
            kT = khT_r.next()
            op("dve", I_copy(kT[:, :, :], bTv[:, 0:512].rearrange("p (h t) -> p h t", t=128)),
               writes=[bT[1], kT])
            return am, kT

        def hg_chain(pr, am, kT):
            c0 = pr * 128
            st0, st1 = hst[2 * pr], hst[2 * pr + 1]
            paired = st0 is st1
            bO = banks[2 + pr % 2]
            op("pe", [I_mm(bO[0][:, hh * 128:(hh + 1) * 128], v_tm[:, pr, hh * 128:(hh + 1) * 128], am[:, hh, :],
                           hh == 0, False, skip=True) for hh in range(4)],
               reads=[v_tm.cells(pr * 512, 512), am], writes=[bO[1]])
            for c in range(2):
                b_ = banks[4 + c]
                op("pe", [I_mm(b_[0][:, hh * 128:(hh + 1) * 128], kT[c * 64:(c + 1) * 64, hh, :],
                               v_tm[c * 64:(c + 1) * 64, pr, hh * 128:(hh + 1) * 128], True, True)
                          for hh in range(4)],
                   reads=[kT, v_tm.cells(pr * 512, 512)], writes=[b_[1]])
            sb0 = st0.sbfs[st0.i % 2]
            sb1 = st1.sbfs[st1.i % 2]
            fns = []
            for hh in range(4):
                fns.append(I_mm(bO[0][:, hh * 128:hh * 128 + 64], sb0[:, hh, :], qt[:, hh, c0:c0 + 64], False, False, skip=True))
                if paired:
                    fns.append(I_mm(bO[0][:, hh * 128 + 64:(hh + 1) * 128], sb0[:, hh, :], qp[:, hh, pr * 64:(pr + 1) * 64],
                                    False, hh == 3, skip=True))
                else:
                    fns.append(I_mm(bO[0][:, hh * 128 + 64:(hh + 1) * 128], sb1[:, hh, :], qt[:, hh, c0 + 64:c0 + 128],
                                    False, hh == 3, skip=True))
            op("pe", fns, reads=[sb0, sb1, qt, qp], writes=[bO[1]])
            for half in range(2):
                for hh in (2 * half, 2 * half + 1):
                    for c in range(2):
                        st = st0 if c == 0 else st1
                        gc = pr * 2 + c
                        b_ = banks[4 + c]
                        sv = st.S32[:, hh, :]
                        op("dve", I_stt(sv, sv, ebl[:, hh, gc, :], b_[0][:, hh * 128:(hh + 1) * 128], ALU.mult, ALU.add),
                           reads=[st.S32.cells(hh * 128, 128), ebl], writes=[b_[1], st.S32.cells(hh * 128, 128)])
                for st in ([st0] if paired else [st0, st1]):
                    nsb = st.sbfs[(st.i + 1) % 2]
                    h0 = 2 * half
                    op("act", I_act(nsb[:, h0:h0 + 2, :], st.S32[:, h0:h0 + 2, :], AF.Copy),
                       reads=[st.S32.cells(h0 * 128, 256)], writes=[nsb.cells(h0 * 128, 256)])
            st0.i += 1
            if not paired:
                st1.i += 1

        def hg_tail_a(pr):
            bO = banks[2 + pr % 2]
            osq = osq_r.next()
            osb = osb_r.next()
            op("act", I_act(osq[:, :], bO[0][:, :], AF.Square), writes=[bO[1], osq])
            op("act", I_act(osb[:, :], bO[0][:, :], AF.Copy), writes=[bO[1], osb])
            return osq, osb

        def hg_tail(pr, pre=None):
            c0 = pr * 128
            bO = banks[2 + pr % 2]
            if pre is not None:
                osq, osb = pre
            else:
                osq, osb = osq_r.next(), None
                op("act", I_act(osq[:, :], bO[0][:, :], AF.Square), writes=[bO[1], osq])
            bV = banks[6]
            op("pe", I_mm(bV[0][:, :], ones_dvb[:, :], osq[:, :], True, True), reads=[ones_dvb, osq], writes=[bV[1]])
            op("act", I_act(tv2[:, :], bV[0][:, :], AF.Ln, bias=epsc[:, 0:1]), reads=[epsc], writes=[bV[1], tv2])
            op("act", I_act(rs2[:, :], tv2[:, :], AF.Exp, scale=-0.5), reads=[tv2], writes=[rs2])
            if osb is not None:
                op("dve", I_tt(t1[:, :], osb[:, :], rs2[:, :], ALU.mult), reads=[osb, rs2], writes=[t1])
            else:
                op("dve", I_tt(t1[:, :], bO[0][:, :], rs2[:, :], ALU.mult), reads=[rs2], writes=[bO[1], t1])
            op("dve", I_tt(r_fm[:, :, c0:c0 + 128], t1[:, :].rearrange("p (h t) -> p h t", t=128),
                           gsn[:, :, c0:c0 + 128], ALU.mult), reads=[t1, gsn], writes=[r_fm])

        fr = hg_front(0)
        pre_prev = None
        for pr in range(NG):
            fr_next = hg_front(pr + 1) if pr + 1 < NG else None
            hg_chain(pr, fr[0], fr[1])
            if NG == 4:
                if pr == 0:
                    ln_stage(0)
                    ln_stage(1)
                elif pr == 1:
                    ln_stage(2)
                    ln_stage(3)
                if 0 < pr < NG - 1:
                    hg_tail(pr - 1)
                elif pr == NG - 1:
                    pre_prev = hg_tail_a(pr - 1)
            else:
                for k in range(4):
                    ln_stage(k)
            fr = fr_next
        if last:
            done = []
            for gc in range(nchunk):
                st = hst[gc]
                if id(st) in done:
                    continue
                done.append(id(st))
                ctx.dma("pool", out_hg[len(done) - 1].rearrange("h k v -> k h v"), st.S32[:, :, :], reads=[st.S32])
        wo = [load_slab(("wout", h)) for h in range(2)]

        def wout_mm(tg, bank_pair=None):
            bks = []
            for h in range(2):
                cs, v = wo[h]
                bk = bank_pair[h] if bank_pair is not None else nb()
                bks.append(bk)
                fns = []
                for kc in range(8):
                    src = c_fm if kc < 4 else r_fm
                    fns.append(I_mm(bk[0][:, :], src[:, kc % 4, tg * 128:(tg + 1) * 128], v[:, kc, :], kc == 0, kc == 7))
                op("pe", fns, reads=[cs] + [c_fm.cells(m * 512 + tg * 128, 128) for m in range(4)]
                   + [r_fm.cells(m * 512 + tg * 128, 128) for m in range(4)], writes=[bk[1]])
            return bks

        def wout_res(tg, bks):
            for h in range(2):
                bk = bks[h]
                xv = xt[:, tg, h * 512:(h + 1) * 512]
                op("dve", I_tt(xv, xv, bk[0][:, :], ALU.add),
                   reads=[xt.cells(tg * D + h * 512, 512)], writes=[bk[1], xt.cells(tg * D + h * 512, 512)])
            if after_tg is not None:
                after_tg(tg)

        pre_last = hg_tail_a(NG - 1)
        if NG == 4:
            b0 = wout_mm(0, (banks[0], banks[1]))
            b1 = wout_mm(1, (banks[2], banks[3]))
            hg_tail(2, pre_prev)
            hg_tail(3, pre_last)
            wout_res(0, b0)
            wout_res(1, b1)
            b2 = wout_mm(2, (banks[4], banks[5]))
            if before_res is not None:
                before_res([banks[7], banks[6]])
            wout_res(2, b2)
            b3 = wout_mm(3, (banks[0], banks[1]))
            if before_res is not None:
                before_res([banks[7]])
            wout_res(3, b3)
        else:
            for tg in range(NG):
                if tg == NG - 1:
                    hg_tail(NG - 1, pre_last)
                bks = wout_mm(tg)
                if before_res is not None:
                    before_res()
                wout_res(tg, bks)

    def final_tg(xt, tg):
        op("dve", I_memset(ss[:, 4 + tg:5 + tg], 0.0), writes=[ss])
        op("act", I_act(fjunk[:, :], xt[:, tg, :], AF.Square, accum_out=ss[:, 4 + tg:5 + tg]),
           reads=[xt.cells(tg * D, D)], writes=[fjunk, ss])
        op("act", I_act(ms[:, 4 + tg:5 + tg], ss[:, 4 + tg:5 + tg], AF.Ln, scale=1.0 / D, bias=epsc[:, 0:1]),
           reads=[ss, epsc], writes=[ms])
        op("act", I_act(rstd[:, 4 + tg:5 + tg], ms[:, 4 + tg:5 + tg], AF.Exp, scale=-0.5), reads=[ms], writes=[rstd])
        xv = xt[:, tg, :]
        op("dve", I_stt(xv, xv, rstd[:, 4 + tg:5 + tg], gbc["final_norm"][:, :], ALU.mult, ALU.mult),
           reads=[xt.cells(tg * D, D), rstd, gbc["final_norm"]], writes=[xt.cells(tg * D, D)])

    tiles = []
    nt_seq = SEQ // 512
    for b_ in range(NP):
        for ti in range(nt_seq):
            tiles.append(("p", b_, ti))
    if do_sample:
        tiles.append(("s", 0, 0))

    def load_x(idx):
        kind, b_, ti = tiles[idx]
        xt = xts[idx % 2]
        if kind == "p":
            src = xp[b_, ti * 512:(ti + 1) * 512, :].rearrange("(tg p) d -> p tg d", p=128)
            ctx.dma("sp", xt[:, 0:4, :], src, writes=[xt])
        else:
            src = xs.rearrange("s t d -> (s t) d")
            ctx.dma("sp", xt[:, 0, :], src, writes=[xt.cells(0, D)])

    hs_main = HState(S32s[0], Sbfs[0])
    load_x(0)
    for idx, (kind, b_, ti) in enumerate(tiles):
        xt = xts[idx % 2]
        if kind == "p":
            NG = 4
            if ti == 0:
                op("pool", I_memset(ubuf[:, :, 0:30], 0.0), writes=[ubuf])
                op("pool", I_memset(S32s[0][:, :, :], 0.0), writes=[S32s[0]])
                op("pool", I_memset(hs_main.cur()[:, :, :], 0.0), writes=[hs_main.cur()])
            segs = [(ubuf, 542, 0, 512)]
            hst = [hs_main] * 8
            last = (ti == nt_seq - 1)
            out_conv = [ncp[b_]]
            out_hg = [nhp[b_]]
            out_y = yp[b_, ti * 512:(ti + 1) * 512, :].rearrange("(tg p) d -> p tg d", p=128)
        else:
            NG = 1
            hst = []
            for s in range(NS):
                ctx.dma("sp", stgc[0:30, :], sconv[s], writes=[stgc])
                bk = nb()
                op("pe", [I_tr(bk[0][:, m * 32:m * 32 + 30], stgc[0:30, m * 128:(m + 1) * 128], ident[0:30, 0:30])
                          for m in range(4)], reads=[stgc, ident], writes=[bk[1]])
                op("act", I_act(ubS[s][:, :, 0:30], bk[0][:, 0:128].rearrange("p (m t) -> p m t", t=32)[:, :, 0:30], AF.Copy),
                   writes=[bk[1], ubS[s]])
                ctx.dma("sp", S32s[s][:, :, :], shg[s].rearrange("h k v -> k h v"), writes=[S32s[s]])
                hs = HState(S32s[s], Sbfs[s])
                op("act", I_act(hs.cur()[:, :, :], S32s[s][:, :, :], AF.Copy), reads=[S32s[s]], writes=[hs.cur()])
                hst.append(hs)
            segs = [(ubS[s], 94, s * 64, 64) for s in range(NS)]
            last = True
            out_conv = [ncs[s] for s in range(NS)]
            out_hg = [nhs[s] for s in range(NS)]
            out_y = ys.rearrange("s t d -> (s t) d")
        nxt = tiles[idx + 1] if idx + 1 < len(tiles) else None
        NGn = (4 if nxt[0] == "p" else 1) if nxt else 0
        xtn = xts[(idx + 1) % 2]
        if idx == 0:
            norm_stage(xt, NG, "ffn1_norm")
        ffn_up(1, NG, after_group=(lambda g: build_diag(g) if g < 4 else None) if idx == 0 else None)
        pend = []

        def _flush(bank_list=None):
            while pend:
                norm_back(pend.pop(0), bank_list.pop(0) if bank_list else None)

        def _ap1(pp, tgs):
            for tg in tgs:
                norm_front(xt, tg, "mix_norm")
                pend.append(tg)

        ffn_down(1, xt, NG, after_mm=lambda pp: _flush(), after_pass=_ap1)

        def _at(tg):
            norm_front(xt, tg, "ffn2_norm")
            pend.append(tg)

        mixer(xt, NG, segs, hst, last, out_conv, out_hg, after_tg=_at, before_res=_flush, first_hook=_flush)
        if nxt:
            load_x(idx + 1)
        ffn_up(2, NG, split=True, mid_hook=_flush)
        for tg in range(NGn):
            norm_front(xtn, tg, "ffn1_norm")

        def _after_mm(pp):
            if pp == 0:
                for tg in range(NGn):
                    norm_back(tg)

        ffn_down(2, xt, NG, after_mm=_after_mm, after_pass=lambda pp, tgs: [final_tg(xt, tg) for tg in tgs])
        ctx.dma("pool", out_y, xt[:, 0:4, :] if kind == "p" else xt[:, 0, :], reads=[xt.cells(0, NG * D)])

    ctx.emit()
    return nc, ctx


_CACHE = {}


def _get_program(SEQ):
    if SEQ not in _CACHE:
        _CACHE[SEQ] = build_program(SEQ)[0]
    return _CACHE[SEQ]


def kernel(**inputs):
    inp = {k: np.ascontiguousarray(np.asarray(v)) for k, v in inputs.items()}
    xp = inp["x_prompt"]
    B, SEQ, _ = xp.shape
    per = B // N_CORES
    nc = _get_program(SEQ)
    shared = {}
    shared["ffn1_w1"] = inp["ffn1_w1"][0]
    shared["ffn1_w3"] = inp["ffn1_w3"][0]
    shared["ffn1_w2"] = inp["ffn1_w2"][0]
    shared["ffn2_w1"] = inp["ffn2_w1"][0]
    shared["ffn2_w3"] = inp["ffn2_w3"][0]
    shared["ffn2_w2"] = inp["ffn2_w2"][0]
    shared["w_in"] = inp["w_in"][0]
    shared["w_out"] = inp["w_out"][0]
    shared["ffn1_norm"] = inp["ffn1_norm"][0]
    shared["mix_norm"] = inp["mix_norm"][0]
    shared["ffn2_norm"] = inp["ffn2_norm"][0]
    shared["final_norm"] = inp["final_norm"]
    shared["conv_dw_w"] = inp["conv_dw_w"][0]
    shared["conv_dw_b"] = inp["conv_dw_b"][0]
    shared["conv_ln_g"] = inp["conv_ln_g"][0]
    shared["conv_ln_b"] = inp["conv_ln_b"][0]
    shared["hg_lb_logits"] = inp["hg_lb_logits"]
    shared["hg_gnorm"] = inp["hg_gnorm"][0]
    in_maps = []
    for c in range(N_CORES):
        m = dict(shared)
        sl = slice(c * per, (c + 1) * per)
        m["xp"] = xp[sl]
        m["xs"] = inp["x_sample"][sl]
        m["sconv"] = inp["state_conv"][0, sl]
        m["shg"] = inp["state_hgrn"][0, sl]
        in_maps.append(m)
    res = run_bass_kernel_spmd(nc, in_maps, core_ids=list(range(N_CORES)))
    R = res.results
    y_prompt = np.concatenate([r["yp"] for r in R], axis=0)
    y_sample = np.concatenate([r["ys"] for r in R], axis=0)
    ncp = np.concatenate([r["ncp"] for r in R], axis=0)[None]
    nhp = np.concatenate([r["nhp"] for r in R], axis=0)[None]
    ncs = np.concatenate([r["ncs"] for r in R], axis=0)[None]
    nhs = np.concatenate([r["nhs"] for r in R], axis=0)[None]
    return (y_prompt.astype(np.float32), y_sample.astype(np.float32), ncp.astype(np.float32),
            nhp.astype(np.float32), ncs.astype(np.float32), nhs.astype(np.float32))
```

```python
import contextlib
import numpy as np
import concourse.bass as bass
import concourse.mybir as mybir
from concourse.bass_utils import run_bass_kernel_spmd

F32 = mybir.dt.float32
BF16 = mybir.dt.bfloat16
U8 = mybir.dt.uint8
AF = mybir.ActivationFunctionType
ALU = mybir.AluOpType

CELL = 256
DT_SIZE = {F32: 4, BF16: 2, U8: 1}

D = 1024
DFF = 2816
NJ = 22
DC = 512
EPS = 1e-6
N_CORES = 8


class Cell:
    __slots__ = ("last_w", "readers")

    def __init__(self):
        self.last_w = None
        self.readers = {}


class Tile:
    def __init__(self, ctx, name, shape, dtype, addr):
        self.ctx = ctx
        self.name = name
        self.shape = list(shape)
        self.dtype = dtype
        self.esz = DT_SIZE[dtype]
        self.free = int(np.prod(shape[1:]))
        self.nbytes = self.free * self.esz
        self.addr = addr
        v = ctx.arena[0:shape[0], addr:addr + self.nbytes].bitcast(dtype)
        if len(shape) == 3:
            v = v.rearrange("p (a b) -> p a b", b=shape[2])
        elif len(shape) == 4:
            v = v.rearrange("p (a b c) -> p a b c", b=shape[2], c=shape[3])
        self.ap = v

    def __getitem__(self, k):
        return self.ap[k]

    def cells(self, lo=None, n=None):
        if lo is None:
            lo, n = 0, self.free
        b0 = self.addr + lo * self.esz
        b1 = self.addr + (lo + n) * self.esz
        return self.ctx.cells[b0 // CELL:(b1 - 1) // CELL + 1]


class Rot:
    def __init__(self, tiles):
        self.tiles = tiles
        self.i = 0

    def next(self):
        t = self.tiles[self.i % len(self.tiles)]
        self.i += 1
        return t


class Ctx:
    ENGS = ["pe", "act", "dve", "pool", "sp"]

    def __init__(self, nc, arena_bytes, n_dma_sems):
        self.nc = nc
        self.arena_bytes = arena_bytes
        self.arena_t = nc.alloc_sbuf_tensor("arena", [128, arena_bytes], U8)
        self.arena = self.arena_t[:, :]
        self.cells = [Cell() for _ in range(arena_bytes // CELL + 1)]
        self.ptr = 0
        self.streams = {e: [] for e in self.ENGS}
        self.cnt = {e: 0 for e in self.ENGS}
        self.clock = {e: {} for e in self.ENGS}
        self.evclock = {}
        self.dma_pool = {q: ["d_%s_%d" % (q, i) for i in range(n)] for q, n in n_dma_sems.items()}
        self.dma_cnt = {}
        for q in self.dma_pool:
            for s in self.dma_pool[q]:
                self.dma_cnt[s] = 0
        self.dma_rr = {q: 0 for q in self.dma_pool}
        self.ninst = 0

    def alloc(self, name, shape, dtype, at=None):
        esz = DT_SIZE[dtype]
        nbytes = int(np.prod(shape[1:])) * esz
        if at is None:
            addr = (self.ptr + CELL - 1) // CELL * CELL
            self.ptr = addr + nbytes
            assert self.ptr <= self.arena_bytes, "SBUF arena overflow at %s: %d" % (name, self.ptr)
        else:
            addr = at
            assert addr + nbytes <= self.arena_bytes
        return Tile(self, name, shape, dtype, addr)

    def _merge_clock(self, eng, k, v):
        ck = self.clock[eng]
        ec = self.evclock.get((k, v))
        if ec:
            for kk, vv in ec.items():
                if ck.get(kk, 0) < vv:
                    ck[kk] = vv
        if ck.get(k, 0) < v:
            ck[k] = v

    def _deps(self, eng, reads, writes, is_dma):
        deps = {}

        def need(ev, kind):
            if ev is None:
                return
            k, v = ev
            if k == eng and not is_dma and eng == "pe":
                return
            if deps.get(k, 0) < v:
                deps[k] = v

        for c in reads:
            need(c.last_w, "raw")
        for c in writes:
            need(c.last_w, "waw")
            for k, v in c.readers.items():
                need((k, v), "war")
        ck = self.clock[eng]
        waits = [(k, v) for k, v in deps.items() if ck.get(k, 0) < v]
        for k, v in waits:
            self._merge_clock(eng, k, v)
        return waits

    def _mark(self, ev, reads, writes):
        k, v = ev
        for c in reads:
            if c.readers.get(k, 0) < v:
                c.readers[k] = v
        for c in writes:
            c.last_w = ev
            c.readers = {}

    @staticmethod
    def _flat(lst):
        out = []
        for x in lst:
            if isinstance(x, Cell):
                out.append(x)
            elif isinstance(x, Tile):
                out.extend(x.cells())
            else:
                out.extend(Ctx._flat(x))
        return out

    def op(self, eng, fns, reads=(), writes=()):
        if callable(fns):
            fns = [fns]
        reads = self._flat(reads)
        writes = self._flat(writes)
        waits = self._deps(eng, reads, writes, False)
        self.cnt[eng] += 1
        ev = (eng, self.cnt[eng])
        self.evclock[ev] = dict(self.clock[eng])
        self._mark(ev, reads, writes)
        self.streams[eng].append(("op", waits, fns, None))
        self.ninst += len(fns) + len(waits)
        return ev

    def dma(self, q, out_ap, in_ap, reads=(), writes=()):
        reads = self._flat(reads)
        writes = self._flat(writes)
        pool = self.dma_pool[q]
        sem = pool[self.dma_rr[q] % len(pool)]
        self.dma_rr[q] += 1
        waits = self._deps(q, reads, writes, True)
        prev = self.dma_cnt[sem]
        if prev > 0 and self.clock[q].get(sem, 0) < 16 * prev:
            waits.append((sem, 16 * prev))
            self._merge_clock(q, sem, 16 * prev)
        self.dma_cnt[sem] += 1
        ev = (sem, 16 * self.dma_cnt[sem])
        self.evclock[ev] = dict(self.clock[q])
        self._mark(ev, reads, writes)
        self.streams[q].append(("dma", waits, [lambda e, o=out_ap, i=in_ap: e.dma_start(out=o, in_=i)], sem))
        self.ninst += 1 + len(waits)
        return ev

    def emit(self):
        nc = self.nc
        with contextlib.ExitStack() as st:
            S = {}
            for e in self.ENGS:
                S[e] = st.enter_context(nc.semaphore("s_" + e))
            for q in self.dma_pool:
                for s in self.dma_pool[q]:
                    S[s] = st.enter_context(nc.semaphore(s))
            block = st.enter_context(nc.Block())
            finals = {e: [] for e in self.ENGS}
            for q in self.dma_pool:
                for s in self.dma_pool[q]:
                    if self.dma_cnt[s] > 0:
                        finals[q].append((s, 16 * self.dma_cnt[s]))

            def run(eng_name, eobj):
                for kind, waits, fns, sem in self.streams[eng_name]:
                    for k, v in waits:
                        eobj.wait_ge(S[k], v)
                    ins = None
                    for f in fns:
                        ins = f(eobj)
                    if kind == "op":
                        ins.then_inc(S[eng_name], 1)
                    else:
                        ins.then_inc(S[sem], 16)
                for k, v in finals[eng_name]:
                    eobj.wait_ge(S[k], v)

            @block.tensor
            def _(t):
                run("pe", t)

            @block.scalar
            def _(a):
                run("act", a)

            @block.vector
            def _(v):
                run("dve", v)

            @block.gpsimd
            def _(g):
                run("pool", g)

            @block.sync
            def _(s):
                run("sp", s)


def I_act(out, in_, func, **kw):
    return lambda e: e.activation(out=out, in_=in_, func=func, **kw)


def I_tt(out, in0, in1, op):
    return lambda e: e.tensor_tensor(out=out, in0=in0, in1=in1, op=op)


def I_ts(out, in0, s1, s2, op0, op1=None):
    if op1 is None:
        return lambda e: e.tensor_scalar(out=out, in0=in0, scalar1=s1, scalar2=None, op0=op0)
    return lambda e: e.tensor_scalar(out=out, in0=in0, scalar1=s1, scalar2=s2, op0=op0, op1=op1)


def I_stt(out, in0, scalar, in1, op0, op1):
    return lambda e: e.scalar_tensor_tensor(out=out, in0=in0, scalar=scalar, in1=in1, op0=op0, op1=op1)


def I_copy(out, in_):
    return lambda e: e.tensor_copy(out=out, in_=in_)


def I_memset(ap, val):
    return lambda e: e.memset(ap, val)


def I_mm(out, lhsT, rhs, start, stop, skip=False):
    if skip:
        return lambda e: e.matmul(out, lhsT=lhsT, rhs=rhs, start=start, stop=stop, skip_group_check=True)
    return lambda e: e.matmul(out, lhsT=lhsT, rhs=rhs, start=start, stop=stop)


def I_tr(out, in_, ident):
    return lambda e: e.transpose(out=out, in_=in_, identity=ident)


def I_scan(out, data):
    return lambda e: e.tensor_tensor_scan(out=out, data0=data, data1=data, initial=0.0,
                                          op0=ALU.add, op1=ALU.bypass)


class HState:
    def __init__(self, S32, sbfs):
        self.S32 = S32
        self.sbfs = sbfs
        self.i = 0

    def cur(self):
        return self.sbfs[self.i % 2]

    def nxt(self):
        self.i += 1
        return self.sbfs[self.i % 2]


def build_program(SEQ, NP=2, NS=2, NR=5, do_sample=True, KS=31, KD=23):
    assert SEQ % 512 == 0
    nc = bass.Bass("TRN2", target_bir_lowering=False)

    def din(name, shape):
        return nc.dram_tensor(name, shape, F32, kind="ExternalInput").ap()

    def dout(name, shape):
        return nc.dram_tensor(name, shape, F32, kind="ExternalOutput").ap()

    xp = din("xp", [NP, SEQ, D])
    xs = din("xs", [NS, 64, D])
    sconv = din("sconv", [NS, 30, DC])
    shg = din("shg", [NS, 4, 128, 128])
    W = {}
    for f in (1, 2):
        W["w1", f] = din("ffn%d_w1" % f, [D, DFF])
        W["w3", f] = din("ffn%d_w3" % f, [D, DFF])
        W["w2", f] = din("ffn%d_w2" % f, [DFF, D])
    w_in = din("w_in", [D, 3072])
    w_out = din("w_out", [D, D])
    gains = {k: din(k, [D]) for k in ("ffn1_norm", "mix_norm", "ffn2_norm", "final_norm")}
    dw_w = din("conv_dw_w", [31, DC])
    dw_b = din("conv_dw_b", [DC])
    ln_g = din("conv_ln_g", [DC])
    ln_b = din("conv_ln_b", [DC])
    lbl = din("hg_lb_logits", [2, DC])
    gn = din("hg_gnorm", [DC])

    yp = dout("yp", [NP, SEQ, D])
    ys = dout("ys", [NS, 64, D])
    ncp = dout("ncp", [NP, 30, DC])
    nhp = dout("nhp", [NP, 4, 128, 128])
    ncs = dout("ncs", [NS, 30, DC])
    nhs = dout("nhs", [NS, 4, 128, 128])

    ARENA = 207 * 1024
    ctx = Ctx(nc, ARENA, {"sp": 24, "pool": 24, "act": 4})
    banks = []
    for i in range(8):
        t = nc.alloc_psum_tensor("bank%d" % i, [128, 512], F32)
        banks.append((t, Cell()))
    bank_i = [0]

    def nb():
        b = banks[bank_i[0] % 8]
        bank_i[0] += 1
        return b

    slabs = {}

    def new_slab(key, a, b):
        slabs[key] = (len(slabs), a, b, Cell())

    for f in (1, 2):
        for g in range(6):
            nch = 4 if g < 5 else 2
            new_slab(("w1", f, g), 8, nch * 128)
            new_slab(("w3", f, g), 8, nch * 128)
        for g in range(6):
            njc = 4 if g < 5 else 2
            new_slab(("w2", f, g), njc, 1024)
    for s in range(6):
        new_slab(("win", s), 8, 512)
    for h in range(2):
        new_slab(("wout", h), 8, 512)
    for m in range(4):
        new_slab(("cv", m), 31, 128)
    NSLAB = len(slabs)
    wsc = nc.dram_tensor("wsc", [NSLAB, 128, 4096], BF16, kind="Internal").ap()

    def slab_dram(key):
        idx, a, b, _ = slabs[key]
        return wsc[idx][:, 0:a * b].rearrange("p (a b) -> p a b", b=b)

    A0, G0, Q0, F0, I0, GG0 = 0, 512, 1024, 1536, 2048, 2560
    win_cols = {
        0: [A0, A0 + 128, G0, G0 + 128],
        1: [A0 + 256, A0 + 384, G0 + 256, G0 + 384],
        2: [Q0, Q0 + 128, F0, F0 + 128],
        3: [Q0 + 256, Q0 + 384, F0 + 256, F0 + 384],
        4: [I0, I0 + 128, I0 + 256, I0 + 384],
        5: [GG0, GG0 + 128, GG0 + 256, GG0 + 384],
    }

    def fp32_pieces(key):
        idx, a, b, cell = slabs[key]
        kind = key[0]
        if kind in ("w1", "w3"):
            _, f, g = key
            return [(0, b, W[kind, f].rearrange("(kc p) n -> p kc n", p=128)[:, :, g * 512:g * 512 + b])]
        if kind == "w2":
            _, f, g = key
            return [(0, b, W["w2", f][g * 512:g * 512 + a * 128, :].rearrange("(jc p) n -> p jc n", p=128))]
        if kind == "wout":
            h = key[1]
            return [(0, b, w_out.rearrange("(kc p) n -> p kc n", p=128)[:, :, h * 512:(h + 1) * 512])]
        assert kind == "win"
        cols = win_cols[key[1]]
        srcv = w_in.rearrange("(kc p) n -> p kc n", p=128)
        out = []
        i = 0
        while i < 4:
            j = i
            while j + 1 < 4 and cols[j + 1] == cols[j] + 128:
                j += 1
            n = (j - i + 1) * 128
            out.append((i * 128, n, srcv[:, :, cols[i]:cols[i] + n]))
            i = j + 1
        return out

    converted = set()

    al = ctx.alloc
    ones = al("ones", [128, 128], F32)
    ident = al("ident", [128, 128], F32)
    identb = al("identb", [128, 128], BF16)
    mask4 = al("mask4", [128, 4, 128], F32)
    ones_dv = al("ones_dv", [128, 128], F32)
    ones_c = al("ones_c", [128, 128], F32)
    epsc = al("epsc", [128, 8], F32)
    cw = al("cw", [128, 124], F32)
    pv = al("pv", [128, 24], F32)
    lbp = al("lbp", [128, 16], F32)
    stg1 = al("stg1", [128, 128], F32)
    stg2 = al("stg2", [128, 128], F32)
    ss = al("ss", [128, 8], F32)
    ms = al("ms", [128, 8], F32)
    rstd = al("rstd", [128, 8], F32)
    ebl = al("ebl", [128, 4, 8, 1], F32)
    gbc = {k: al("gbc_" + k, [128, D], F32) for k in gains}
    xts = [al("xt%d" % i, [128, 4, D], F32) for i in range(2)]
    ubuf = al("ubuf", [128, 4, 542], BF16)
    u32s = [al("u32l%d" % i, [128, 4, 32], F32) for i in range(2)]
    ubS = [al("ubS%d" % i, [128, 4, 94], BF16) for i in range(NS)]
    S32s = [al("S32_%d" % i, [128, 4, 128], F32) for i in range(2)]
    Sbfs = [[al("Sbf_%d_%d" % (i, j), [128, 4, 128], BF16) for j in range(2)] for i in range(2)]
    v_tm = al("v_tm", [128, 4, 512], BF16)
    gsn = al("gsn", [128, 4, 512], F32)
    r_fm = al("r_fm", [128, 4, 512], BF16)
    ring = [al("ring%d" % i, [128, 4096], BF16) for i in range(NR)]
    HT = ctx.ptr = (ctx.ptr + CELL - 1) // CELL * CELL
    h_tm = al("h_tm", [128, 4, D], BF16)
    qt = al("qt", [128, 4, 512], BF16, at=HT)
    kt = al("kt", [128, 4, 512], BF16, at=HT + 4096)
    HF = ctx.ptr = (ctx.ptr + CELL - 1) // CELL * CELL
    h_fm = al("h_fm", [128, 8, 512], BF16)
    kh = al("kh", [128, 4, 512], BF16, at=HF)
    c_fm = al("c_fm", [128, 4, 512], BF16, at=HF + 4096)
    MT = ctx.ptr = (ctx.ptr + CELL - 1) // CELL * CELL
    g_fm = al("g_fm", [128, NJ, 512], BF16)
    sa = Rot([al("sa%d" % i, [128, 512], F32) for i in range(2)])
    p = MT
    ah_t = []
    for i in range(3):
        ah_t.append(al("ah%d" % i, [128, 512], F32, at=p)); p += 2048
    th_t = []
    for i in range(2):
        th_t.append(al("th%d" % i, [128, 512], F32, at=p)); p += 2048
    qs = al("qs", [128, 4, 512], F32, at=p); p += 8192
    thf = al("thf", [128, 4, 512], F32, at=p); p += 8192
    kk = al("kk", [128, 4, 512], F32, at=p); p += 8192
    b_t = []
    for i in range(2):
        b_t.append(al("bb%d" % i, [128, 512], F32, at=p)); p += 2048
    E_t = []
    for i in range(3):
        E_t.append(al("E%d" % i, [128, 512], F32, at=p)); p += 2048
    D_t = []
    for i in range(2):
        D_t.append(al("Dd%d" % i, [128, 512], F32, at=p)); p += 2048
    MT_END1 = p
    p = MT
    ycv = al("ycv", [128, 4, 512], F32, at=p); p += 8192
    p += 4096
    p += 4096
    tt_t = []
    for i in range(2):
        tt_t.append(al("tt%d" % i, [128, 512], F32, at=p)); p += 2048
    Am_t = []
    for i in range(2):
        Am_t.append(al("Am%d" % i, [128, 4, 128], BF16, at=p)); p += 1024
    khT_t = []
    for i in range(2):
        khT_t.append(al("khT%d" % i, [128, 4, 128], BF16, at=p)); p += 1024
    osq_t = []
    for i in range(2):
        osq_t.append(al("osq%d" % i, [128, 512], BF16, at=p)); p += 2048
    osb_t = []
    for i in range(2):
        osb_t.append(al("osb%d" % i, [128, 512], F32, at=p)); p += 2048
    tv2 = al("tv2", [128, 512], F32, at=p); p += 2048
    rs2 = al("rs2", [128, 512], F32, at=p); p += 2048
    t1 = al("t1", [128, 512], F32, at=p); p += 2048
    MT_END2 = p
    ctx.ptr = max(ctx.ptr, MT_END1, MT_END2)
    dgs = al("dgs", [128, 31, 128], BF16)
    fjunk = al("fjunk", [128, D], BF16, at=dgs.addr)
    tv = al("tv", [128, 512], F32, at=dgs.addr + 2048)
    stgc = al("stgc", [128, 512], F32, at=dgs.addr + 2048)
    qp = al("qp", [128, 4, 256], BF16)
    ones_dvb = al("ones_dvb", [128, 128], BF16)
    ones_cb = al("ones_cb", [128, 128], BF16)
    rs = al("rs", [128, 512], F32, at=dgs.addr + 4096)
    sq_t = [al("sqA", [128, 512], BF16, at=MT + 8192), al("sqB", [128, 512], BF16)]
    b_t.append(al("bb2", [128, 512], F32))
    D_t.append(al("Dd2", [128, 512], F32))
    assert ctx.ptr <= ARENA, ctx.ptr
    tt4 = [tt_t[0], tt_t[1], osq_t[0], osq_t[1]]
    ah_r, th_r, b_r, E_r, D_r = Rot(ah_t), Rot(th_t), Rot(b_t), Rot(E_t), Rot(D_t)
    sq_r, tt_r, Am_r, khT_r, osq_r, osb_r = Rot(sq_t), Rot(tt_t), Rot(Am_t), Rot(khT_t), Rot(osq_t), Rot(osb_t)

    op = ctx.op
    op("pool", I_memset(ones[:, :], 1.0), writes=[ones])
    op("pool", lambda e: e.affine_select(out=ident[:, :], in_=ones[:, :], pattern=[[-1, 128]],
                                         compare_op=ALU.is_equal, fill=0.0, base=0, channel_multiplier=1),
       reads=[ones], writes=[ident])
    for hh in range(4):
        op("pool", lambda e, hh=hh: e.affine_select(out=mask4[:, hh, :], in_=ones[:, :], pattern=[[1, 128]],
                                                     compare_op=ALU.is_ge, fill=0.0, base=0, channel_multiplier=-1),
           reads=[ones], writes=[mask4])
    op("pool", I_memset(ones_dv[:, :], 1.0 / 128), writes=[ones_dv])
    op("pool", I_memset(ones_c[:, :], 1.0 / 512), writes=[ones_c])
    op("pool", I_memset(ones_dvb[:, :], 1.0 / 128), writes=[ones_dvb])
    op("pool", I_memset(ones_cb[:, :], 1.0 / 512), writes=[ones_cb])
    op("pool", I_memset(epsc[:, :], EPS), writes=[epsc])
    op("dve", I_copy(identb[:, :], ident[:, :]), reads=[ident], writes=[identb])
    ctx.dma("sp", stg1[0:124, :], dw_w.rearrange("k (c p) -> (k c) p", p=128), writes=[stg1])
    for r0, vec in ((0, dw_b), (4, ln_g), (8, ln_b), (12, gn)):
        ctx.dma("sp", stg2[r0:r0 + 4, :], vec.rearrange("(c p) -> c p", p=128), writes=[stg2])
    ctx.dma("sp", stg2[16:24, :], lbl.rearrange("r (c p) -> (r c) p", p=128), writes=[stg2])
    for k, g_ap in gains.items():
        ctx.dma("sp", gbc[k][:, :], g_ap.partition_broadcast(128), writes=[gbc[k]])
    b = nb()
    op("pe", I_tr(b[0][:, 0:124], stg1[0:124, :], ident[0:124, 0:124]), reads=[stg1, ident], writes=[b[1]])
    op("dve", I_copy(cw[:, :], b[0][:, 0:124]), writes=[b[1], cw])
    b = nb()
    op("pe", I_tr(b[0][:, 0:24], stg2[0:24, :], ident[0:24, 0:24]), reads=[stg2, ident], writes=[b[1]])
    op("dve", I_copy(pv[:, :], b[0][:, 0:24]), writes=[b[1], pv])
    op("dve", I_tt(lbp[:, 0:4], pv[:, 16:20], pv[:, 20:24], ALU.subtract), reads=[pv], writes=[lbp])
    op("act", I_act(lbp[:, 0:4], lbp[:, 0:4], AF.Tanh, scale=0.5), reads=[lbp], writes=[lbp])
    op("dve", I_ts(lbp[:, 4:8], lbp[:, 0:4], -0.25, 0.25, ALU.mult, ALU.add), reads=[lbp], writes=[lbp])
    op("dve", I_ts(lbp[:, 8:12], lbp[:, 0:4], 0.25, 0.75, ALU.mult, ALU.add), reads=[lbp], writes=[lbp])
    op("dve", I_ts(lbp[:, 12:16], lbp[:, 0:4], 0.25, -0.25, ALU.mult, ALU.add), reads=[lbp], writes=[lbp])

    def C1(m):
        return lbp[:, 4 + m:5 + m]

    def C0(m):
        return lbp[:, 8 + m:9 + m]

    def NC1(m):
        return lbp[:, 12 + m:13 + m]

    def build_diag(m):
        for k in range(31):
            op("act", I_act(dgs[:, k, :], identb[:, :], AF.Identity, scale=cw[:, k * 4 + m:k * 4 + m + 1]),
               reads=[identb, cw], writes=[dgs.cells(k * 128, 128)])
        ctx.dma("act", slab_dram(("cv", m)), dgs[:, :, :], reads=[dgs], writes=[slabs[("cv", m)][3]])

    ring_i = [0]

    def load_slab(key):
        idx, a, b_, cell = slabs[key]
        t = ring[ring_i[0] % NR]
        ring_i[0] += 1
        dst = t.ap[:, 0:a * b_].rearrange("p (a b) -> p a b", b=b_)
        cells = t.cells(0, a * b_)
        if key[0] != "cv" and key not in converted:
            converted.add(key)
            for (c_lo, n, src) in fp32_pieces(key):
                ctx.dma("pool", dst[:, :, c_lo:c_lo + n], src, writes=[cells])
            ctx.dma("sp", slab_dram(key), dst, reads=[cells], writes=[cell])
        else:
            ctx.dma("sp", dst, slab_dram(key), reads=[cell], writes=[cells])
        return cells, dst

    def norm_front(xt, tg, gk):
        op("dve", I_memset(ss[:, tg:tg + 1], 0.0), writes=[ss])
        op("act", I_act(h_tm[:, tg, :], xt[:, tg, :], AF.Square, accum_out=ss[:, tg:tg + 1]),
           reads=[xt.cells(tg * D, D)], writes=[h_tm.cells(tg * D, D), ss])
        op("act", I_act(ms[:, tg:tg + 1], ss[:, tg:tg + 1], AF.Ln, scale=1.0 / D, bias=epsc[:, 0:1]),
           reads=[ss, epsc], writes=[ms])
        op("act", I_act(rstd[:, tg:tg + 1], ms[:, tg:tg + 1], AF.Exp, scale=-0.5), reads=[ms], writes=[rstd])
        op("dve", I_stt(h_tm[:, tg, :], xt[:, tg, :], rstd[:, tg:tg + 1], gbc[gk][:, :], ALU.mult, ALU.mult),
           reads=[xt.cells(tg * D, D), rstd, gbc[gk]], writes=[h_tm.cells(tg * D, D)])

    def norm_back(tg, bank=None):
        bk = bank if bank is not None else nb()
        bv = bk[0][:, :].bitcast(BF16)
        op("pe", [I_tr(bv[:, fc * 128:(fc + 1) * 128], h_tm[:, tg, fc * 128:(fc + 1) * 128], identb[:, :])
                  for fc in range(8)], reads=[h_tm.cells(tg * D, D), identb], writes=[bk[1]])
        src = bv[:, :].rearrange("p (f t) -> p f t", t=128)
        dst = h_fm[:, :, tg * 128:(tg + 1) * 128]
        hc = [h_fm.cells(fc * 512 + tg * 128, 128) for fc in range(8)]
        if tg % 2 == 0:
            op("act", I_act(dst, src, AF.Copy), writes=[bk[1], hc])
        else:
            op("dve", I_copy(dst, src), writes=[bk[1], hc])

    def norm_stage(xt, NG, gk):
        for tg in range(NG):
            norm_front(xt, tg, gk)
        for tg in range(NG):
            norm_back(tg)

    def ffn_up(f, NG, after_group=None, split=False, mid_hook=None):
        ntok = NG * 128
        for g in range(6):
            nch = 4 if g < 5 else 2
            c1_, v1 = load_slab(("w1", f, g))
            c3_, v3 = load_slab(("w3", f, g))
            if g == 0 and not (split and NG == 4) and mid_hook is not None:
                mid_hook()
            if split and g == 0 and NG == 4:
                nsp = 3
                bks = [(nb(), nb()) for i in range(nsp)]
                for (lo, n) in ((0, 384), (384, 128)):
                    if lo > 0 and mid_hook is not None:
                        mid_hook()
                    hc = [h_fm.cells(kc * 512 + lo, n) for kc in range(8)]
                    for i in range(nsp):
                        bA, bB = bks[i]
                        op("pe", [I_mm(bA[0][:, lo:lo + n], v1[:, kc, i * 128:(i + 1) * 128], h_fm[:, kc, lo:lo + n],
                                       kc == 0, kc == 7) for kc in range(8)], reads=[c1_, hc], writes=[bA[1]])
                        op("pe", [I_mm(bB[0][:, lo:lo + n], v3[:, kc, i * 128:(i + 1) * 128], h_fm[:, kc, lo:lo + n],
                                       kc == 0, kc == 7) for kc in range(8)], reads=[c3_, hc], writes=[bB[1]])
                for i in range(nch):
                    j = g * 4 + i
                    if i < nsp:
                        bA, bB = bks[i]
                    else:
                        bA = nb()
                        bB = nb()
                        op("pe", [I_mm(bA[0][:, 0:ntok], v1[:, kc, i * 128:(i + 1) * 128], h_fm[:, kc, 0:ntok], kc == 0, kc == 7)
                                  for kc in range(8)], reads=[c1_, h_fm], writes=[bA[1]])
                        op("pe", [I_mm(bB[0][:, 0:ntok], v3[:, kc, i * 128:(i + 1) * 128], h_fm[:, kc, 0:ntok], kc == 0, kc == 7)
                                  for kc in range(8)], reads=[c3_, h_fm], writes=[bB[1]])
                    s = sa.next()
                    op("act", I_act(s[:, 0:ntok], bA[0][:, 0:ntok], AF.Silu), writes=[bA[1], s])
                    op("dve", I_tt(g_fm[:, j, 0:ntok], s[:, 0:ntok], bB[0][:, 0:ntok], ALU.mult),
                       reads=[s], writes=[bB[1], g_fm.cells(j * 512, ntok)])
                if after_group is not None:
                    after_group(g)
                continue
            for i in range(nch):
                j = g * 4 + i
                bA = nb()
                bB = nb()
                op("pe", [I_mm(bA[0][:, 0:ntok], v1[:, kc, i * 128:(i + 1) * 128], h_fm[:, kc, 0:ntok], kc == 0, kc == 7)
                          for kc in range(8)], reads=[c1_, h_fm], writes=[bA[1]])
                op("pe", [I_mm(bB[0][:, 0:ntok], v3[:, kc, i * 128:(i + 1) * 128], h_fm[:, kc, 0:ntok], kc == 0, kc == 7)
                          for kc in range(8)], reads=[c3_, h_fm], writes=[bB[1]])
                s = sa.next()
                op("act", I_act(s[:, 0:ntok], bA[0][:, 0:ntok], AF.Silu), writes=[bA[1], s])
                op("dve", I_tt(g_fm[:, j, 0:ntok], s[:, 0:ntok], bB[0][:, 0:ntok], ALU.mult),
                   reads=[s], writes=[bB[1], g_fm.cells(j * 512, ntok)])
            if after_group is not None:
                after_group(g)

    def ffn_down(f, xt, NG, after_mm=None, after_pass=None):
        if NG == 4:
            passes = [[0, 1, 2], [3]] if f == 1 else [[0, 1], [2, 3]]
        else:
            passes = [list(range(NG))]
        for pp, tgs in enumerate(passes):
            accs = {}
            for tg in tgs:
                for ch in range(2):
                    accs[(tg, ch)] = nb()
            for g in range(6):
                njc = 4 if g < 5 else 2
                c2_, v2 = load_slab(("w2", f, g))
                fns = []
                for jc in range(njc):
                    j = g * 4 + jc
                    for tg in tgs:
                        for ch in range(2):
                            fns.append(I_mm(accs[(tg, ch)][0][:, :], g_fm[:, j, tg * 128:(tg + 1) * 128],
                                            v2[:, jc, ch * 512:(ch + 1) * 512], j == 0, j == NJ - 1))
                op("pe", fns, reads=[c2_, g_fm], writes=[a[1] for a in accs.values()])
            for (tg, ch), a in accs.items():
                xv = xt[:, tg, ch * 512:(ch + 1) * 512]
                op("dve", I_stt(xv, a[0][:, :], 0.5, xv, ALU.mult, ALU.add),
                   reads=[xt.cells(tg * D + ch * 512, 512)], writes=[a[1], xt.cells(tg * D + ch * 512, 512)])
            if after_mm is not None:
                after_mm(pp)
            if after_pass is not None:
                after_pass(pp, tgs)

    def mixer(xt, NG, segs, hst, last, out_conv, out_hg, after_tg=None, before_res=None, first_hook=None):
        ntok = NG * 128
        nchunk = ntok // 64
        for sl in range(2):
            cs, v = load_slab(("win", sl))
            ahs = []
            pre_bks = None
            if sl == 0 and NG != 4 and first_hook is not None:
                first_hook()
            if sl == 0 and NG == 4:
                pre_bks = [nb() for i in range(4)]
                for (lo, n) in ((0, 384), (384, 128)):
                    if lo > 0 and first_hook is not None:
                        first_hook()
                    hc = [h_fm.cells(kc * 512 + lo, n) for kc in range(8)]
                    for i in range(4):
                        bk = pre_bks[i]
                        op("pe", [I_mm(bk[0][:, lo:lo + n], v[:, kc, i * 128:(i + 1) * 128], h_fm[:, kc, lo:lo + n],
                                       kc == 0, kc == 7) for kc in range(8)], reads=[cs, hc], writes=[bk[1]])
            for i in range(4):
                if pre_bks is not None:
                    bk = pre_bks[i]
                else:
                    bk = nb()
                    op("pe", [I_mm(bk[0][:, 0:ntok], v[:, kc, i * 128:(i + 1) * 128], h_fm[:, kc, 0:ntok], kc == 0, kc == 7)
                              for kc in range(8)], reads=[cs, h_fm], writes=[bk[1]])
                if i < 2:
                    a = ah_r.next()
                    ahs.append(a)
                    op("act", I_act(a[:, 0:ntok], bk[0][:, 0:ntok], AF.Identity, scale=0.5), writes=[bk[1], a])
                else:
                    m = 2 * sl + (i - 2)
                    h = th_r.next()
                    op("act", I_act(h[:, 0:ntok], bk[0][:, 0:ntok], AF.Tanh, scale=0.5), writes=[bk[1], h])
                    for si, (ub, Wd, c0, L) in enumerate(segs):
                        op("dve", I_stt(ub[:, m, 30:30 + L], h[:, c0:c0 + L], 1.0, ahs[i - 2][:, c0:c0 + L], ALU.add, ALU.mult),
                           reads=[h, ahs[i - 2]], writes=[ub.cells(m * Wd + 30, L)])
                        if last:
                            e0 = c0 + L - 30
                            op("dve", I_stt(u32s[si][:, m, 0:30], h[:, e0:e0 + 30], 1.0, ahs[i - 2][:, e0:e0 + 30], ALU.add, ALU.mult),
                               reads=[h, ahs[i - 2]], writes=[u32s[si]])
        dve_bg = []
        for (ub, Wd, c0, L) in segs:
            for k in range(KD, 31):
                for m in range(4):
                    yv = ycv[:, m, c0:c0 + L]
                    if k == KD:
                        dve_bg.append((I_ts(yv, ub[:, m, k:k + L], cw[:, k * 4 + m:k * 4 + m + 1], None, ALU.mult),
                                       [ub.cells(m * Wd, Wd), cw], [ycv.cells(m * 512 + c0, L)]))
                    else:
                        dve_bg.append((I_stt(yv, ub[:, m, k:k + L], cw[:, k * 4 + m:k * 4 + m + 1], yv, ALU.mult, ALU.add),
                                       [ub.cells(m * Wd, Wd), ycv.cells(m * 512 + c0, L), cw], [ycv.cells(m * 512 + c0, L)]))

        def bgd(n):
            for _ in range(n):
                if dve_bg:
                    f_, r_, w_ = dve_bg.pop(0)
                    op("dve", f_, reads=r_, writes=w_)

        for sl in (2, 3):
            cs, v = load_slab(("win", sl))
            for i in range(4):
                m = 2 * (sl - 2) + i % 2
                bk = nb()
                op("pe", [I_mm(bk[0][:, 0:ntok], v[:, kc, i * 128:(i + 1) * 128], h_fm[:, kc, 0:ntok], kc == 0, kc == 7)
                          for kc in range(8)], reads=[cs, h_fm], writes=[bk[1]])
                if i < 2:
                    op("act", I_act(qs[:, m, 0:ntok], bk[0][:, 0:ntok], AF.Silu), writes=[bk[1], qs.cells(m * 512, ntok)])
                else:
                    op("act", I_act(thf[:, m, 0:ntok], bk[0][:, 0:ntok], AF.Tanh, scale=0.5),
                       writes=[bk[1], thf.cells(m * 512, ntok)])
                    op("dve", I_ts(kk[:, m, 0:ntok], thf[:, m, 0:ntok], NC1(m), C1(m), ALU.mult, ALU.add),
                       reads=[thf.cells(m * 512, ntok), lbp], writes=[kk.cells(m * 512, ntok)])
                bgd(4)
        cs, v = load_slab(("win", 5))
        for m in range(4):
            bk = nb()
            op("pe", [I_mm(bk[0][:, 0:ntok], v[:, kc, m * 128:(m + 1) * 128], h_fm[:, kc, 0:ntok], kc == 0, kc == 7)
                      for kc in range(8)], reads=[cs, h_fm], writes=[bk[1]])
            op("act", I_act(gsn[:, m, 0:ntok], bk[0][:, 0:ntok], AF.Silu), writes=[bk[1], gsn.cells(m * 512, ntok)])
            op("dve", I_ts(gsn[:, m, 0:ntok], gsn[:, m, 0:ntok], pv[:, 12 + m:13 + m], None, ALU.mult),
               reads=[gsn.cells(m * 512, ntok), pv], writes=[gsn.cells(m * 512, ntok)])
            bgd(2)
        cs, v = load_slab(("win", 4))
        for tg in range(NG):
            bk = nb()
            op("pe", [I_mm(bk[0][:, :], h_fm[:, kc, tg * 128:(tg + 1) * 128], v[:, kc, :], kc == 0, kc == 7)
                      for kc in range(8)], reads=[cs, h_fm], writes=[bk[1]])
            op("act", I_act(v_tm[:, tg, :], bk[0][:, :], AF.Copy), writes=[bk[1], v_tm.cells(tg * 512, 512)])
            bgd(2)
        bgd(10 ** 6)
        for m in range(4):
            fm = thf[:, m, 0:ntok]
            fcell = thf.cells(m * 512, ntok)
            op("act", I_act(fm, fm, AF.Ln, scale=C1(m), bias=C0(m)), reads=[fcell, lbp], writes=[fcell])
        bbs, dds = {}, {}

        def prep_scan(m):
            fcell = thf.cells(m * 512, ntok)
            bb = b_r.next()
            bbs[m] = bb
            for c in range(nchunk):
                op("dve", I_scan(bb[:, c * 64:(c + 1) * 64], thf[:, m, c * 64:(c + 1) * 64]),
                   reads=[fcell], writes=[bb.cells(c * 64, 64)])
            bb3 = bb[:, 0:ntok].rearrange("p (c j) -> p c j", j=64)
            dd = D_r.next()
            dds[m] = dd
            dd3 = dd[:, 0:ntok].rearrange("p (c j) -> p c j", j=64)
            op("dve", I_tt(dd3, bb3[:, :, 63:64].to_broadcast([128, nchunk, 64]), bb3, ALU.subtract),
               reads=[bb], writes=[dd])

        def prep_exp(m):
            bb, dd = bbs[m], dds[m]
            bb3 = bb[:, 0:ntok].rearrange("p (c j) -> p c j", j=64)
            e1 = E_r.next()
            op("act", I_act(e1[:, 0:ntok], bb[:, 0:ntok], AF.Exp), reads=[bb], writes=[e1])
            e2 = E_r.next()
            op("act", I_act(e2[:, 0:ntok], bb[:, 0:ntok], AF.Exp, scale=-1.0), reads=[bb], writes=[e2])
            e3 = E_r.next()
            op("act", I_act(e3[:, 0:ntok], dd[:, 0:ntok], AF.Exp), reads=[dd], writes=[e3])
            op("act", I_act(ebl[:, m, 0:nchunk, :], bb3[:, :, 63:64], AF.Exp), reads=[bb], writes=[ebl])
            op("dve", I_tt(qt[:, m, 0:ntok], qs[:, m, 0:ntok], e1[:, 0:ntok], ALU.mult),
               reads=[qs.cells(m * 512, ntok), e1], writes=[qt.cells(m * 512, ntok)])
            op("dve", I_tt(kt[:, m, 0:ntok], kk[:, m, 0:ntok], e2[:, 0:ntok], ALU.mult),
               reads=[kk.cells(m * 512, ntok), e2], writes=[kt.cells(m * 512, ntok)])
            op("dve", I_tt(kh[:, m, 0:ntok], kk[:, m, 0:ntok], e3[:, 0:ntok], ALU.mult),
               reads=[kk.cells(m * 512, ntok), e3], writes=[kh.cells(m * 512, ntok)])
            npair = ntok // 128
            q4 = qt[:, m, 0:ntok].rearrange("p (a two j) -> p a two j", two=2, j=64)
            e4 = ebl[:, m, 0:nchunk, :].rearrange("p (a two) o -> p a two o", two=2)
            op("dve", I_tt(qp[:, m, 0:npair * 64].rearrange("p (a j) -> p a j", j=64), q4[:, :, 1, :],
                           e4[:, :, 0, :].to_broadcast([128, npair, 64]), ALU.mult),
               reads=[qt.cells(m * 512, ntok), ebl], writes=[qp.cells(m * 256, npair * 64)])

        prep_scan(0)
        prep_scan(1)
        prep_exp(0)
        prep_scan(2)
        prep_exp(1)
        prep_scan(3)
        prep_exp(2)
        prep_exp(3)
        for m in range(4):
            cs, v = load_slab(("cv", m))
            bk = nb()
            fns = []
            first = True
            for (ub, Wd, c0, L) in segs:
                for k in range(KD):
                    fns.append(I_mm(bk[0][:, c0:c0 + L], v[:, k, :], ub[:, m, k:k + L], first,
                                    (KD == 31 and k == KD - 1 and c0 + L == ntok), skip=True))
                    first = False
            rds = [cs] + [ub.cells(m * Wd, Wd) for (ub, Wd, c0, L) in segs]
            if KD < 31:
                fns.append(I_mm(bk[0][:, 0:ntok], ident[:, :], ycv[:, m, 0:ntok], False, True, skip=True))
                rds += [ident, ycv.cells(m * 512, ntok)]
            op("pe", fns, reads=rds, writes=[bk[1]])
            op("act", I_act(ycv[:, m, 0:ntok], bk[0][:, 0:ntok], AF.Identity, bias=pv[:, m:m + 1]),
               reads=[pv], writes=[bk[1], ycv.cells(m * 512, ntok)])
        if last:
            pass

        def bg(n):
            return

        for si, (ub, Wd, c0, L) in enumerate(segs):
            if not last:
                op("pool", I_copy(ub[:, :, 0:30], ub[:, :, L:L + 30]), reads=[ub], writes=[ub])
            else:
                bk = nb()
                op("pe", [I_tr(bk[0][0:30, m * 128:(m + 1) * 128], u32s[si][:, m, 0:30], ident[:, :]) for m in range(4)],
                   reads=[u32s[si], ident], writes=[bk[1]])
                op("act", I_act(stgc[0:30, :], bk[0][0:30, :], AF.Copy), writes=[bk[1], stgc])
                ctx.dma("pool", out_conv[si], stgc[0:30, :], reads=[stgc])
        def ln_stage(k):
            if k == 0:
                bm = banks[6]
                for m in range(4):
                    op("pe", I_mm(bm[0][:, 0:ntok], ones_c[:, :], ycv[:, m, 0:ntok], m == 0, m == 3),
                       reads=[ones_c, ycv.cells(m * 512, ntok)], writes=[bm[1]])
                for m in range(4):
                    yv = ycv[:, m, 0:ntok]
                    op("dve", I_tt(yv, yv, bm[0][:, 0:ntok], ALU.subtract),
                       reads=[ycv.cells(m * 512, ntok)], writes=[bm[1], ycv.cells(m * 512, ntok)])
            elif k == 1:
                bvv = banks[7]
                for m in range(4):
                    sq = sq_r.next()
                    op("act", I_act(sq[:, 0:ntok], ycv[:, m, 0:ntok], AF.Square), reads=[ycv.cells(m * 512, ntok)], writes=[sq])
                    op("pe", I_mm(bvv[0][:, 0:ntok], ones_cb[:, :], sq[:, 0:ntok], m == 0, m == 3),
                       reads=[ones_cb, sq], writes=[bvv[1]])
                op("act", I_act(tv[:, 0:ntok], bvv[0][:, 0:ntok], AF.Ln, bias=epsc[:, 0:1]), reads=[epsc], writes=[bvv[1], tv])
                op("act", I_act(rs[:, 0:ntok], tv[:, 0:ntok], AF.Exp, scale=-0.5), reads=[tv], writes=[rs])
            elif k == 2:
                for m in range(4):
                    yv = ycv[:, m, 0:ntok]
                    op("dve", I_tt(yv, yv, rs[:, 0:ntok], ALU.mult),
                       reads=[ycv.cells(m * 512, ntok), rs], writes=[ycv.cells(m * 512, ntok)])
            else:
                for m in range(4):
                    op("act", I_act(c_fm[:, m, 0:ntok], ycv[:, m, 0:ntok], AF.Silu, scale=pv[:, 4 + m:5 + m],
                                    bias=pv[:, 8 + m:9 + m]),
                       reads=[ycv.cells(m * 512, ntok), pv], writes=[c_fm.cells(m * 512, ntok)])

        def hg_front(pr):
            c0 = pr * 128
            paired = hst[2 * pr] is hst[2 * pr + 1]
            bA = banks[0]
            fns = []
            for hh in range(4):
                fns.append(I_mm(bA[0][:, hh * 128:(hh + 1) * 128], kt[:, hh, c0:c0 + 128], qt[:, hh, c0:c0 + 128], True, True))
                if paired:
                    fns.append(I_mm(bA[0][0:64, hh * 128 + 64:(hh + 1) * 128], kh[:, hh, c0:c0 + 64],
                                    qt[:, hh, c0 + 64:c0 + 128], True, True))
            op("pe", fns, reads=[kt, qt, kh], writes=[bA[1]])
            am = Am_r.next()
            op("dve", I_tt(am[:, :, :], bA[0][:, :].rearrange("p (h t) -> p h t", t=128), mask4[:, :, :], ALU.mult),
               reads=[mask4], writes=[bA[1], am])
            if not paired:
                op("dve", I_memset(am[0:64, :, 64:128], 0.0), writes=[am])
            bT = banks[1]
            bTv = bT[0][:, :].bitcast(BF16)
            op("pe", [I_tr(bTv[:, hh * 128:(hh + 1) * 128], kh[:, hh, c0:c0 + 128], identb[:, :]) for hh in range(4)],
               reads=[kh, identb], writes=[bT[1]])
            kT = khT_r.next()
            op("dve", I_copy(kT[:, :, :], bTv[:, 0:512].rearrange("p (h t) -> p h t", t=128)),
               writes=[bT[1], kT])
            return am, kT

        def hg_chain(pr, am, kT):
            c0 = pr * 128
            st0, st1 = hst[2 * pr], hst[2 * pr + 1]
            paired = st0 is st1
            bO = banks[2 + pr % 2]
            op("pe", [I_mm(bO[0][:, hh * 128:(hh + 1) * 128], v_tm[:, pr, hh * 128:(hh + 1) * 128], am[:, hh, :],
                           hh == 0, False, skip=True) for hh in range(4)],
               reads=[v_tm.cells(pr * 512, 512), am], writes=[bO[1]])
            for c in range(2):
                b_ = banks[4 + c]
                op("pe", [I_mm(b_[0][:, hh * 128:(hh + 1) * 128], kT[c * 64:(c + 1) * 64, hh, :],
                               v_tm[c * 64:(c + 1) * 64, pr, hh * 128:(hh + 1) * 128], True, True)
                          for hh in range(4)],
                   reads=[kT, v_tm.cells(pr * 512, 512)], writes=[b_[1]])
            sb0 = st0.sbfs[st0.i % 2]
            sb1 = st1.sbfs[st1.i % 2]
            fns = []
            for hh in range(4):
                fns.append(I_mm(bO[0][:, hh * 128:hh * 128 + 64], sb0[:, hh, :], qt[:, hh, c0:c0 + 64], False, False, skip=True))
                if paired:
                    fns.append(I_mm(bO[0][:, hh * 128 + 64:(hh + 1) * 128], sb0[:, hh, :], qp[:, hh, pr * 64:(pr + 1) * 64],
                                    False, hh == 3, skip=True))
                else:
                    fns.append(I_mm(bO[0][:, hh * 128 + 64:(hh + 1) * 128], sb1[:, hh, :], qt[:, hh, c0 + 64:c0 + 128],
                                    False, hh == 3, skip=True))
            op("pe", fns, reads=[sb0, sb1, qt, qp], writes=[bO[1]])
            for half in range(2):
                for hh in (2 * half, 2 * half + 1):
                    for c in range(2):
                        st = st0 if c == 0 else st1
                        gc = pr * 2 + c
                        b_ = banks[4 + c]
                        sv = st.S32[:, hh, :]
                        op("dve", I_stt(sv, sv, ebl[:, hh, gc, :], b_[0][:, hh * 128:(hh + 1) * 128], ALU.mult, ALU.add),
                           reads=[st.S32.cells(hh * 128, 128), ebl], writes=[b_[1], st.S32.cells(hh * 128, 128)])
                for st in ([st0] if paired else [st0, st1]):
                    nsb = st.sbfs[(st.i + 1) % 2]
                    h0 = 2 * half
                    op("act", I_act(nsb[:, h0:h0 + 2, :], st.S32[:, h0:h0 + 2, :], AF.Copy),
                       reads=[st.S32.cells(h0 * 128, 256)], writes=[nsb.cells(h0 * 128, 256)])
            st0.i += 1
            if not paired:
                st1.i += 1

        def hg_tail_a(pr):
            bO = banks[2 + pr % 2]
            osq = osq_r.next()
            osb = osb_r.next()
            op("act", I_act(osq[:, :], bO[0][:, :], AF.Square), writes=[bO[1], osq])
            op("act", I_act(osb[:, :], bO[0][:, :], AF.Copy), writes=[bO[1], osb])
            return osq, osb

        def hg_tail(pr, pre=None):
            c0 = pr * 128
            osq, osb = pre if pre is not None else hg_tail_a(pr)
            bV = banks[6]
            op("pe", I_mm(bV[0][:, :], ones_dvb[:, :], osq[:, :], True, True), reads=[ones_dvb, osq], writes=[bV[1]])
            op("act", I_act(tv2[:, :], bV[0][:, :], AF.Ln, bias=epsc[:, 0:1]), reads=[epsc], writes=[bV[1], tv2])
            op("act", I_act(rs2[:, :], tv2[:, :], AF.Exp, scale=-0.5), reads=[tv2], writes=[rs2])
            op("dve", I_tt(t1[:, :], osb[:, :], rs2[:, :], ALU.mult), reads=[osb, rs2], writes=[t1])
            op("dve", I_tt(r_fm[:, :, c0:c0 + 128], t1[:, :].rearrange("p (h t) -> p h t", t=128),
                           gsn[:, :, c0:c0 + 128], ALU.mult), reads=[t1, gsn], writes=[r_fm])

        fr = hg_front(0)
        pre_prev = None
        for pr in range(NG):
            fr_next = hg_front(pr + 1) if pr + 1 < NG else None
            hg_chain(pr, fr[0], fr[1])
            if NG == 4:
                if pr == 0:
                    ln_stage(0)
                    ln_stage(1)
                elif pr == 1:
                    ln_stage(2)
                    ln_stage(3)
                if 0 < pr < NG - 1:
                    hg_tail(pr - 1)
                elif pr == NG - 1:
                    pre_prev = hg_tail_a(pr - 1)
            else:
                for k in range(4):
                    ln_stage(k)
            fr = fr_next
        if last:
            done = []
            for gc in range(nchunk):
                st = hst[gc]
                if id(st) in done:
                    continue
                done.append(id(st))
                ctx.dma("pool", out_hg[len(done) - 1].rearrange("h k v -> k h v"), st.S32[:, :, :], reads=[st.S32])
        wo = [load_slab(("wout", h)) for h in range(2)]

        def wout_mm(tg, bank_pair=None):
            bks = []
            for h in range(2):
                cs, v = wo[h]
                bk = bank_pair[h] if bank_pair is not None else nb()
                bks.append(bk)
                fns = []
                for kc in range(8):
                    src = c_fm if kc < 4 else r_fm
                    fns.append(I_mm(bk[0][:, :], src[:, kc % 4, tg * 128:(tg + 1) * 128], v[:, kc, :], kc == 0, kc == 7))
                op("pe", fns, reads=[cs] + [c_fm.cells(m * 512 + tg * 128, 128) for m in range(4)]
                   + [r_fm.cells(m * 512 + tg * 128, 128) for m in range(4)], writes=[bk[1]])
            return bks

        def wout_res(tg, bks):
            for h in range(2):
                bk = bks[h]
                xv = xt[:, tg, h * 512:(h + 1) * 512]
                op("dve", I_tt(xv, xv, bk[0][:, :], ALU.add),
                   reads=[xt.cells(tg * D + h * 512, 512)], writes=[bk[1], xt.cells(tg * D + h * 512, 512)])
            if after_tg is not None:
                after_tg(tg)

        pre_last = hg_tail_a(NG - 1)
        if NG == 4:
            b0 = wout_mm(0, (banks[0], banks[1]))
            b1 = wout_mm(1, (banks[2], banks[3]))
            hg_tail(2, pre_prev)
            hg_tail(3, pre_last)
            wout_res(0, b0)
            wout_res(1, b1)
            b2 = wout_mm(2, (banks[4], banks[5]))
            if before_res is not None:
                before_res([banks[7], banks[6]])
            wout_res(2, b2)
            b3 = wout_mm(3, (banks[0], banks[1]))
            if before_res is not None:
                before_res([banks[7]])
            wout_res(3, b3)
        else:
            for tg in range(NG):
                if tg == NG - 1:
                    hg_tail(NG - 1, pre_last)
                bks = wout_mm(tg)
                if before_res is not None:
                    before_res()
                wout_res(tg, bks)

    def final_tg(xt, tg):
        op("dve", I_memset(ss[:, 4 + tg:5 + tg], 0.0), writes=[ss])
        op("act", I_act(fjunk[:, :], xt[:, tg, :], AF.Square, accum_out=ss[:, 4 + tg:5 + tg]),
           reads=[xt.cells(tg * D, D)], writes=[fjunk, ss])
        op("act", I_act(ms[:, 4 + tg:5 + tg], ss[:, 4 + tg:5 + tg], AF.Ln, scale=1.0 / D, bias=epsc[:, 0:1]),
           reads=[ss, epsc], writes=[ms])
        op("act", I_act(rstd[:, 4 + tg:5 + tg], ms[:, 4 + tg:5 + tg], AF.Exp, scale=-0.5), reads=[ms], writes=[rstd])
        xv = xt[:, tg, :]
        op("dve", I_stt(xv, xv, rstd[:, 4 + tg:5 + tg], gbc["final_norm"][:, :], ALU.mult, ALU.mult),
           reads=[xt.cells(tg * D, D), rstd, gbc["final_norm"]], writes=[xt.cells(tg * D, D)])

    tiles = []
    nt_seq = SEQ // 512
    for b_ in range(NP):
        for ti in range(nt_seq):
            tiles.append(("p", b_, ti))
    if do_sample:
        tiles.append(("s", 0, 0))

    def load_x(idx):
        kind, b_, ti = tiles[idx]
        xt = xts[idx % 2]
        if kind == "p":
            src = xp[b_, ti * 512:(ti + 1) * 512, :].rearrange("(tg p) d -> p tg d", p=128)
            ctx.dma("sp", xt[:, 0:4, :], src, writes=[xt])
        else:
            src = xs.rearrange("s t d -> (s t) d")
            ctx.dma("sp", xt[:, 0, :], src, writes=[xt.cells(0, D)])

    hs_main = HState(S32s[0], Sbfs[0])
    load_x(0)
    for idx, (kind, b_, ti) in enumerate(tiles):
        xt = xts[idx % 2]
        if kind == "p":
            NG = 4
            if ti == 0:
                op("pool", I_memset(ubuf[:, :, 0:30], 0.0), writes=[ubuf])
                op("pool", I_memset(S32s[0][:, :, :], 0.0), writes=[S32s[0]])
                op("pool", I_memset(hs_main.cur()[:, :, :], 0.0), writes=[hs_main.cur()])
            segs = [(ubuf, 542, 0, 512)]
            hst = [hs_main] * 8
            last = (ti == nt_seq - 1)
            out_conv = [ncp[b_]]
            out_hg = [nhp[b_]]
            out_y = yp[b_, ti * 512:(ti + 1) * 512, :].rearrange("(tg p) d -> p tg d", p=128)
        else:
            NG = 1
            hst = []
            for s in range(NS):
                ctx.dma("sp", stgc[0:30, :], sconv[s], writes=[stgc])
                bk = nb()
                op("pe", [I_tr(bk[0][:, m * 32:m * 32 + 30], stgc[0:30, m * 128:(m + 1) * 128], ident[0:30, 0:30])
                          for m in range(4)], reads=[stgc, ident], writes=[bk[1]])
                op("act", I_act(ubS[s][:, :, 0:30], bk[0][:, 0:128].rearrange("p (m t) -> p m t", t=32)[:, :, 0:30], AF.Copy),
                   writes=[bk[1], ubS[s]])
                ctx.dma("sp", S32s[s][:, :, :], shg[s].rearrange("h k v -> k h v"), writes=[S32s[s]])
                hs = HState(S32s[s], Sbfs[s])
                op("act", I_act(hs.cur()[:, :, :], S32s[s][:, :, :], AF.Copy), reads=[S32s[s]], writes=[hs.cur()])
                hst.append(hs)
            segs = [(ubS[s], 94, s * 64, 64) for s in range(NS)]
            last = True
            out_conv = [ncs[s] for s in range(NS)]
            out_hg = [nhs[s] for s in range(NS)]
            out_y = ys.rearrange("s t d -> (s t) d")
        nxt = tiles[idx + 1] if idx + 1 < len(tiles) else None
        NGn = (4 if nxt[0] == "p" else 1) if nxt else 0
        xtn = xts[(idx + 1) % 2]
        if idx == 0:
            norm_stage(xt, NG, "ffn1_norm")
        ffn_up(1, NG, after_group=(lambda g: build_diag(g) if g < 4 else None) if idx == 0 else None)
        pend = []

        def _flush(bank_list=None):
            while pend:
                norm_back(pend.pop(0), bank_list.pop(0) if bank_list else None)

        def _ap1(pp, tgs):
            for tg in tgs:
                norm_front(xt, tg, "mix_norm")
                pend.append(tg)

        ffn_down(1, xt, NG, after_mm=lambda pp: _flush(), after_pass=_ap1)

        def _at(tg):
            norm_front(xt, tg, "ffn2_norm")
            pend.append(tg)

        mixer(xt, NG, segs, hst, last, out_conv, out_hg, after_tg=_at, before_res=_flush, first_hook=_flush)
        if nxt:
            load_x(idx + 1)
        ffn_up(2, NG, split=True, mid_hook=_flush)
        for tg in range(NGn):
            norm_front(xtn, tg, "ffn1_norm")

        def _after_mm(pp):
            if pp == 0:
                for tg in range(NGn):
                    norm_back(tg)

        ffn_down(2, xt, NG, after_mm=_after_mm, after_pass=lambda pp, tgs: [final_tg(xt, tg) for tg in tgs])
        ctx.dma("pool", out_y, xt[:, 0:4, :] if kind == "p" else xt[:, 0, :], reads=[xt.cells(0, NG * D)])

    ctx.emit()
    return nc, ctx


_CACHE = {}


def _get_program(SEQ):
    if SEQ not in _CACHE:
        _CACHE[SEQ] = build_program(SEQ)[0]
    return _CACHE[SEQ]


def kernel(**inputs):
    inp = {k: np.ascontiguousarray(np.asarray(v)) for k, v in inputs.items()}
    xp = inp["x_prompt"]
    B, SEQ, _ = xp.shape
    per = B // N_CORES
    nc = _get_program(SEQ)
    shared = {}
    shared["ffn1_w1"] = inp["ffn1_w1"][0]
    shared["ffn1_w3"] = inp["ffn1_w3"][0]
    shared["ffn1_w2"] = inp["ffn1_w2"][0]
    shared["ffn2_w1"] = inp["ffn2_w1"][0]
    shared["ffn2_w3"] = inp["ffn2_w3"][0]
    shared["ffn2_w2"] = inp["ffn2_w2"][0]
    shared["w_in"] = inp["w_in"][0]
    shared["w_out"] = inp["w_out"][0]
    shared["ffn1_norm"] = inp["ffn1_norm"][0]
    shared["mix_norm"] = inp["mix_norm"][0]
    shared["ffn2_norm"] = inp["ffn2_norm"][0]
    shared["final_norm"] = inp["final_norm"]
    shared["conv_dw_w"] = inp["conv_dw_w"][0]
    shared["conv_dw_b"] = inp["conv_dw_b"][0]
    shared["conv_ln_g"] = inp["conv_ln_g"][0]
    shared["conv_ln_b"] = inp["conv_ln_b"][0]
    shared["hg_lb_logits"] = inp["hg_lb_logits"]
    shared["hg_gnorm"] = inp["hg_gnorm"][0]
    in_maps = []
    for c in range(N_CORES):
        m = dict(shared)
        sl = slice(c * per, (c + 1) * per)
        m["xp"] = xp[sl]
        m["xs"] = inp["x_sample"][sl]
        m["sconv"] = inp["state_conv"][0, sl]
        m["shg"] = inp["state_hgrn"][0, sl]
        in_maps.append(m)
    res = run_bass_kernel_spmd(nc, in_maps, core_ids=list(range(N_CORES)))
    R = res.results
    y_prompt = np.concatenate([r["yp"] for r in R], axis=0)
    y_sample = np.concatenate([r["ys"] for r in R], axis=0)
    ncp = np.concatenate([r["ncp"] for r in R], axis=0)[None]
    nhp = np.concatenate([r["nhp"] for r in R], axis=0)[None]
    ncs = np.concatenate([r["ncs"] for r in R], axis=0)[None]
    nhs = np.concatenate([r["nhs"] for r in R], axis=0)[None]
    return (y_prompt.astype(np.float32), y_sample.astype(np.float32), ncp.astype(np.float32),
            nhp.astype(np.float32), ncs.astype(np.float32), nhs.astype(np.float32))
```
